# Optimizing a Trainium2 kernel written in Bass

```python
import jax, jax.numpy as jnp
from jax import lax
import numpy as np

D_MODEL = 1024
BATCH = 16
SEQ = 2048
DEPTH = 2
DEC_BATCH = 32
DEC_SEQ = 1
PAST_LEN = 16384
PAGE_SIZE = 128

CHUNK = 128
A_GROUPS = 8
A_GROUP_DIM = 64
A_WIDTH = A_GROUPS * A_GROUP_DIM
HEAD_DIM = 64
HEADS_PER_PAIR = 4
DILATED_PAIRS = ((128, 1), (512, 4), (2048, 16))
N_PAIRS = len(DILATED_PAIRS)
B_HEADS = HEADS_PER_PAIR * N_PAIRS
B_WIDTH = B_HEADS * HEAD_DIM
B_OUT = HEADS_PER_PAIR * HEAD_DIM
QBLK = 128
D_FF = 2816
CONV_W = 3
ROPE_THETA = 10000.0
EPS = 1e-6
NEG_INF = -1e30
IN_COLS = 2 * A_WIDTH + 3 * B_WIDTH + 2 * D_MODEL

kernel_name = "hybrid_gmlp_dilated_attn_convffn_step"


def rmsnorm(x, g):
    xf = x.astype(jnp.float32)
    y = xf * lax.rsqrt(jnp.mean(xf * xf, axis=-1, keepdims=True) + EPS)
    return (y * g.astype(jnp.float32)).astype(x.dtype)


def apply_rope(x, pos):
    half = HEAD_DIM // 2
    inv_freq = ROPE_THETA ** (-jnp.arange(half, dtype=jnp.float32) / half)
    ang = pos.astype(jnp.float32)[:, None] * inv_freq[None, :]
    cos = jnp.cos(ang)[:, None, :]
    sin = jnp.sin(ang)[:, None, :]
    xf = x.astype(jnp.float32)
    x1, x2 = xf[..., :half], xf[..., half:]
    return jnp.concatenate([x1 * cos - x2 * sin, x2 * cos + x1 * sin], axis=-1).astype(x.dtype)


def masked_softmax_lse(scores, valid):
    s = jnp.where(valid, scores, NEG_INF)
    mx = jnp.max(s, axis=-1, keepdims=True)
    p = jnp.exp(s - mx)
    den = jnp.sum(p, axis=-1, keepdims=True)
    return p / den, (mx + jnp.log(den))[..., 0]


def chunk_spatial_gate(u, v, w_s, b_s):
    b, t, _ = v.shape
    tp = -(-t // CHUNK) * CHUNK
    nc = tp // CHUNK
    vp = jnp.pad(v, ((0, 0), (0, tp - t), (0, 0))).reshape(b, nc, CHUNK, A_GROUPS, A_GROUP_DIM)
    causal = jnp.tril(jnp.ones((CHUNK, CHUNK), dtype=bool))
    w = jnp.where(causal[None], w_s, 0)
    z = jnp.einsum('gts,bcsgd->bctgd', w, vp) + b_s.T[None, None, :, :, None]
    z = z.reshape(b, tp, A_WIDTH)[:, :t]
    return u * z


def dilated_attn_prompt(q, k, v, dil, taps):
    b, s, h, hd = q.shape
    m = s // dil
    mp = -(-m // QBLK) * QBLK
    nb = mp // QBLK

    def to_blocks(a):
        a = a.reshape(b, m, dil, h, hd).transpose(0, 2, 1, 3, 4)
        a = jnp.pad(a, ((0, 0), (0, 0), (0, mp - m), (0, 0), (0, 0)))
        return a.reshape(b, dil, nb, QBLK, h, hd)

    def with_prev(a):
        prev = jnp.pad(a, ((0, 0), (0, 0), (1, 0), (0, 0), (0, 0), (0, 0)))[:, :, :-1]
        return jnp.concatenate([prev, a], axis=3)

    qb = to_blocks(q)
    kk = with_prev(to_blocks(k))
    vv = with_prev(to_blocks(v))
    scores = jnp.einsum('brnqhd,brnkhd->brnhqk', qb, kk,
                        preferred_element_type=jnp.float32) * (HEAD_DIM ** -0.5)
    qi = jnp.arange(QBLK)[:, None]
    ki = jnp.arange(2 * QBLK)[None, :]
    dist = qi - ki + QBLK
    key_sub = jnp.arange(nb)[:, None, None] * QBLK + ki[None] - QBLK
    valid = (dist[None] >= 0) & (dist[None] <= taps) & (key_sub >= 0)
    attn, lse = masked_softmax_lse(scores, valid[None, None, :, None])
    out = jnp.einsum('brnhqk,brnkhd->brnqhd', attn, vv.astype(jnp.float32))
    out = out.reshape(b, dil, mp, h, hd)[:, :, :m].transpose(0, 2, 1, 3, 4).reshape(b, s, h, hd)
    lse = lse.transpose(0, 1, 2, 4, 3).reshape(b, dil, mp, h)[:, :, :m]
    lse = lse.transpose(0, 2, 1, 3).reshape(b, s, h)
    return out, lse


def dilated_attn_sample(q, kv_rows, kv_buf, dil, taps):
    b, t, h, hd = q.shape
    n_past = kv_buf.shape[1]
    all_kv = jnp.concatenate([kv_buf, kv_rows], axis=1)
    idx = n_past + jnp.arange(t)[:, None] - jnp.arange(taps + 1)[None, :] * dil
    valid = idx >= 0
    gathered = all_kv[:, jnp.maximum(idx, 0)]
    kg, vg = gathered[:, :, :, 0], gathered[:, :, :, 1]
    scores = jnp.einsum('bthd,btjhd->bthj', q, kg,
                        preferred_element_type=jnp.float32) * (HEAD_DIM ** -0.5)
    attn, lse = masked_softmax_lse(scores, valid[None, :, None, :])
    out = jnp.einsum('bthj,btjhd->bthd', attn, vg.astype(jnp.float32))
    return out, lse


def layer(x, pos, kv_bufs, conv_buf, g_mix, w_in, g_v, w_s, b_s, g_q, g_k,
          w_a_proj, w_b_proj, w_o, g_ffn, w_up, conv_w, conv_b, w_down):
    b, t, _ = x.shape
    xn = rmsnorm(x, g_mix)
    proj = xn @ w_in
    o1 = A_WIDTH
    o2 = o1 + A_WIDTH
    o3 = o2 + B_WIDTH
    o4 = o3 + B_WIDTH
    o5 = o4 + B_WIDTH
    o6 = o5 + D_MODEL
    u_a, v_a, q, k, v, gate_a, gate_b = jnp.split(proj, [o1, o2, o3, o4, o5, o6], axis=-1)

    u = jax.nn.gelu(u_a)
    va = rmsnorm(jax.nn.gelu(v_a), g_v)
    a_out = chunk_spatial_gate(u, va, w_s, b_s)

    q = apply_rope(rmsnorm(q.reshape(b, t, B_HEADS, HEAD_DIM), g_q), pos)
    k = apply_rope(rmsnorm(k.reshape(b, t, B_HEADS, HEAD_DIM), g_k), pos)
    v = v.reshape(b, t, B_HEADS, HEAD_DIM)
    outs, lses, new_kv = [], [], []
    for gi, (window, dil) in enumerate(DILATED_PAIRS):
        sl = slice(gi * HEADS_PER_PAIR, (gi + 1) * HEADS_PER_PAIR)
        qg, kg, vg = q[:, :, sl], k[:, :, sl], v[:, :, sl]
        taps = window // dil
        kv_rows = jnp.stack([kg, vg], axis=2)
        if kv_bufs is None:
            o, lse = dilated_attn_prompt(qg, kg, vg, dil, taps)
            new_kv.append(kv_rows[:, -min(window, t):])
        else:
            o, lse = dilated_attn_sample(qg, kv_rows, kv_bufs[gi], dil, taps)
            new_kv.append(kv_rows)
        outs.append(o)
        lses.append(lse)
    pair_w = jax.nn.softmax(jnp.stack(lses, axis=0), axis=0)
    b_out = sum(pair_w[gi][..., None] * outs[gi] for gi in range(N_PAIRS))
    b_out = b_out.astype(x.dtype).reshape(b, t, B_OUT)

    merged = jax.nn.sigmoid(gate_a) * (a_out @ w_a_proj) + jax.nn.sigmoid(gate_b) * (b_out @ w_b_proj)
    x = x + merged @ w_o

    up = rmsnorm(x, g_ffn) @ w_up
    if conv_buf is None:
        hist = jnp.zeros((b, CONV_W - 1, up.shape[-1]), dtype=up.dtype)
    else:
        hist = conv_buf.astype(up.dtype)
    padded = jnp.concatenate([hist, up], axis=1)
    c = conv_b + sum(conv_w[j] * padded[:, j:j + t] for j in range(CONV_W))
    new_conv = padded[:, -(CONV_W - 1):]
    c_gate, c_val = jnp.split(c, 2, axis=-1)
    x = x + (jax.nn.silu(c_gate) * c_val) @ w_down
    return x, new_kv, new_conv, va


def setup_inputs(seed: int = 0) -> dict:
    key = jax.random.key(seed)
    ks = jax.random.split(key, 24)

    def nrm(k, shape, scale):
        return jax.random.normal(k, shape, jnp.float32) * scale

    buf_len = [min(w, PAST_LEN) for (w, _) in DILATED_PAIRS]
    return {
        "x_prompt": nrm(ks[0], (BATCH, SEQ, D_MODEL), 1.0),
        "x_sample": nrm(ks[1], (DEC_BATCH, DEC_SEQ, D_MODEL), 1.0),
        "cache_kv_w128": nrm(ks[2], (DEPTH, DEC_BATCH, buf_len[0], 2, HEADS_PER_PAIR, HEAD_DIM), 1.0),
        "cache_kv_w512": nrm(ks[3], (DEPTH, DEC_BATCH, buf_len[1], 2, HEADS_PER_PAIR, HEAD_DIM), 1.0),
        "cache_kv_w2048": nrm(ks[4], (DEPTH, DEC_BATCH, buf_len[2], 2, HEADS_PER_PAIR, HEAD_DIM), 1.0),
        "state_conv": nrm(ks[5], (DEPTH, DEC_BATCH, CONV_W - 1, 2 * D_FF), 1.0),
        "g_mix": 1.0 + nrm(ks[6], (DEPTH, D_MODEL), 0.05),
        "w_in": nrm(ks[7], (DEPTH, D_MODEL, IN_COLS), D_MODEL ** -0.5),
        "g_v": 1.0 + nrm(ks[8], (DEPTH, A_WIDTH), 0.05),
        "w_s": nrm(ks[9], (DEPTH, A_GROUPS, CHUNK, CHUNK), CHUNK ** -0.5),
        "b_s": 1.0 + nrm(ks[10], (DEPTH, A_GROUPS, CHUNK), 0.1),
        "g_q": 1.0 + nrm(ks[11], (DEPTH, HEAD_DIM), 0.05),
        "g_k": 1.0 + nrm(ks[12], (DEPTH, HEAD_DIM), 0.05),
        "w_a_proj": nrm(ks[13], (DEPTH, A_WIDTH, D_MODEL), A_WIDTH ** -0.5),
        "w_b_proj": nrm(ks[14], (DEPTH, B_OUT, D_MODEL), B_OUT ** -0.5),
        "w_o": nrm(ks[15], (DEPTH, D_MODEL, D_MODEL), D_MODEL ** -0.5),
        "g_ffn": 1.0 + nrm(ks[16], (DEPTH, D_MODEL), 0.05),
        "w_up": nrm(ks[17], (DEPTH, D_MODEL, 2 * D_FF), D_MODEL ** -0.5),
        "conv_w": nrm(ks[18], (DEPTH, CONV_W, 2 * D_FF), CONV_W ** -0.5),
        "conv_b": nrm(ks[19], (DEPTH, 2 * D_FF), 0.02),
        "w_down": nrm(ks[20], (DEPTH, D_FF, D_MODEL), D_FF ** -0.5),
    }


def reference(x_prompt, x_sample, cache_kv_w128, cache_kv_w512, cache_kv_w2048, state_conv,
              g_mix, w_in, g_v, w_s, b_s, g_q, g_k, w_a_proj, w_b_proj, w_o,
              g_ffn, w_up, conv_w, conv_b, w_down):
    pos_prompt = jnp.arange(x_prompt.shape[1])
    pos_sample = PAST_LEN + jnp.arange(x_sample.shape[1])
    yp, ys = x_prompt, x_sample
    kvp = [[], [], []]
    kvs = [[], [], []]
    convp, convs, vrows = [], [], []
    for l in range(DEPTH):
        params = (g_mix[l], w_in[l], g_v[l], w_s[l], b_s[l], g_q[l], g_k[l],
                  w_a_proj[l], w_b_proj[l], w_o[l], g_ffn[l], w_up[l], conv_w[l], conv_b[l], w_down[l])
        yp, nkv_p, nconv_p, _ = layer(yp, pos_prompt, None, None, *params)
        ys, nkv_s, nconv_s, va_s = layer(
            ys, pos_sample, (cache_kv_w128[l], cache_kv_w512[l], cache_kv_w2048[l]),
            state_conv[l], *params)
        for gi in range(N_PAIRS):
            kvp[gi].append(nkv_p[gi])
            kvs[gi].append(nkv_s[gi])
        convp.append(nconv_p)
        convs.append(nconv_s)
        vrows.append(va_s)
    new_kv_w128_prompt = jnp.stack(kvp[0], axis=0)
    new_kv_w512_prompt = jnp.stack(kvp[1], axis=0)
    new_kv_w2048_prompt = jnp.stack(kvp[2], axis=0)
    new_conv_prompt = jnp.stack(convp, axis=0)
    new_kv_w128_sample = jnp.stack(kvs[0], axis=0)
    new_kv_w512_sample = jnp.stack(kvs[1], axis=0)
    new_kv_w2048_sample = jnp.stack(kvs[2], axis=0)
    new_conv_sample = jnp.stack(convs, axis=0)
    new_v_chunk_sample = jnp.stack(vrows, axis=0)
    return (yp, ys, new_kv_w128_prompt, new_kv_w512_prompt, new_kv_w2048_prompt, new_conv_prompt,
            new_kv_w128_sample, new_kv_w512_sample, new_kv_w2048_sample, new_conv_sample,
            new_v_chunk_sample)
```

```python
import math
from collections import defaultdict
from contextlib import ExitStack

import numpy as np
import concourse.bass as bass
import concourse.mybir as mybir
from concourse.bass_utils import run_bass_kernel_spmd

F32 = mybir.dt.float32
BF16 = mybir.dt.bfloat16
AF = mybir.ActivationFunctionType
ALU = mybir.AluOpType
AX = mybir.AxisListType

D = 1024
T = 2048
L = 2
NSEQ = 2
NS = 4
DFF = 2816
NCH = 22
EPS = 1e-6
NEG = -30000.0
PAIRS = ((128, 1), (512, 4), (2048, 16))
PAST = 16384
NCORES = 8

DEV = {}


def _esz(dt):
    return mybir.dt.size(dt)


def _region(ap):
    t = ap.tensor
    if type(t).__name__.startswith("DRam"):
        return None
    shape = t.shape
    pstep = 1
    for s in shape[1:]:
        pstep *= s
    esz = _esz(ap.dtype)
    off = ap.offset
    p0 = off // pstep
    f0 = off % pstep
    dims = ap.ap
    npart = dims[0][1] if dims[0][0] != 0 else 1
    ext = 1
    for s, c in dims[1:]:
        ext += (c - 1) * abs(s)
    return (type(t).__name__[0] + t.name, p0, p0 + npart, f0 * esz, (f0 + ext) * esz)


class Op:
    __slots__ = ("idx", "engine", "emit", "deps", "dma", "needed", "count", "sem", "val", "prev")

    def __init__(self, idx, engine, emit, dma):
        self.idx = idx
        self.engine = engine
        self.emit = emit
        self.deps = set()
        self.dma = dma
        self.needed = False
        self.count = 0
        self.sem = None
        self.val = 0
        self.prev = None


class Prog:
    ENGS = ("tensor", "scalar", "vector", "gpsimd", "sync")
    NPOOL = 12

    def __init__(self, nc):
        self.nc = nc
        self.ops = []
        self.recs = defaultdict(list)
        self.dry = False

    def add(self, engine, emit, reads=(), writes=(), dma=False):
        if self.dry:
            return None
        op = Op(len(self.ops), engine, emit, dma)
        self.ops.append(op)
        for ap in reads:
            self._access(op, ap, False)
        for ap in writes:
            self._access(op, ap, True)
        return op

    def _access(self, op, ap, is_write):
        reg = _region(ap)
        if reg is None:
            return
        name, p0, p1, lo, hi = reg
        if name[0] == "P":
            p0, p1, lo, hi, is_write = 0, 128, 0, 2048, True
        lst = self.recs[name]
        out = []
        for r in lst:
            if r[1] <= p0 or p1 <= r[0] or r[3] <= lo or hi <= r[2]:
                out.append(r)
                continue
            dop = self.ops[r[4]]
            if dop.idx != op.idx:
                if is_write:
                    same = (dop.engine == op.engine) and not dop.dma and not op.dma and op.engine != "gpsimd"
                    if not same:
                        op.deps.add(dop.idx)
                elif r[5]:
                    op.deps.add(dop.idx)
            covered = r[0] >= p0 and r[1] <= p1 and r[2] >= lo and r[3] <= hi
            if is_write and covered:
                continue
            if (not is_write) and (not r[5]) and covered and dop.engine == op.engine and not dop.dma and not op.dma:
                continue
            out.append(r)
        out.append((p0, p1, lo, hi, op.idx, is_write))
        self.recs[name] = out

    def mm(self, out, lhsT, rhs, start=True, stop=True):
        return self.add("tensor", lambda e: e.matmul(out, lhsT=lhsT, rhs=rhs, start=start, stop=stop),
                        reads=(lhsT, rhs), writes=(out,))

    def tr(self, out, in_, ident):
        return self.add("tensor", lambda e: e.transpose(out=out, in_=in_, identity=ident),
                        reads=(in_, ident), writes=(out,))

    def act(self, out, in_, func, scale=None, bias=None, accum_out=None):
        kw = {}
        reads = [in_]
        writes = [out]
        if scale is not None:
            kw["scale"] = scale
            if not isinstance(scale, (int, float)):
                reads.append(scale)
        if bias is not None:
            kw["bias"] = bias
            if not isinstance(bias, (int, float)):
                reads.append(bias)
        if accum_out is not None:
            kw["accum_out"] = accum_out
            writes.append(accum_out)
        return self.add("scalar", lambda e: e.activation(out=out, in_=in_, func=func, **kw), reads, writes)

    def tt(self, eng, out, in0, in1, op):
        return self.add(eng, lambda e: e.tensor_tensor(out=out, in0=in0, in1=in1, op=op), (in0, in1), (out,))

    def ts(self, eng, out, in0, s1, s2, op0, op1=None):
        reads = [in0]
        for s in (s1, s2):
            if s is not None and not isinstance(s, (int, float)):
                reads.append(s)
        if op1 is None:
            return self.add(eng, lambda e: e.tensor_scalar(out=out, in0=in0, scalar1=s1, scalar2=None, op0=op0),
                            reads, (out,))
        return self.add(eng, lambda e: e.tensor_scalar(out=out, in0=in0, scalar1=s1, scalar2=s2, op0=op0, op1=op1),
                        reads, (out,))

    def stt(self, out, in0, scalar, in1, op0, op1):
        reads = [in0, in1]
        if not isinstance(scalar, (int, float)):
            reads.append(scalar)
        return self.add("vector", lambda e: e.scalar_tensor_tensor(out=out, in0=in0, scalar=scalar, in1=in1,
                                                                    op0=op0, op1=op1), reads, (out,))

    def copy(self, eng, out, in_):
        if eng == "scalar":
            return self.act(out, in_, AF.Copy)
        return self.add(eng, lambda e: e.tensor_copy(out=out, in_=in_), (in_,), (out,))

    def reduce(self, out, in_, op=ALU.add):
        return self.add("vector", lambda e: e.tensor_reduce(out=out, in_=in_, axis=AX.X, op=op), (in_,), (out,))

    def recip(self, out, in_):
        return self.add("vector", lambda e: e.reciprocal(out=out, in_=in_), (in_,), (out,))

    def memset(self, eng, out, val):
        return self.add(eng, lambda e: e.memset(out, val), (), (out,))

    def dma(self, q, out, in_):
        return self.add(q, lambda e: e.dma_start(out=out, in_=in_), (in_,), (out,), dma=True)

    def emit_all(self, es):
        nc = self.nc
        ops = self.ops
        sems = {e: es.enter_context(nc.semaphore("s_" + e)) for e in self.ENGS}
        pools = {q: [es.enter_context(nc.semaphore("d_%s_%d" % (q, i))) for i in range(self.NPOOL)]
                 for q in ("sync", "gpsimd")}
        for op in ops:
            for d in op.deps:
                ops[d].needed = True
        cnt = defaultdict(int)
        didx = defaultdict(int)
        dtot = {q: [0] * self.NPOOL for q in pools}
        dlast = {q: [None] * self.NPOOL for q in pools}
        for op in ops:
            if op.dma:
                q = op.engine
                k = didx[q] % self.NPOOL
                didx[q] += 1
                op.sem = pools[q][k]
                dtot[q][k] += 16
                op.val = dtot[q][k]
                op.prev = dlast[q][k]
                dlast[q][k] = op
            elif op.needed:
                cnt[op.engine] += 1
                op.count = cnt[op.engine]
        per = {e: [op for op in ops if op.engine == e] for e in self.ENGS}
        block = es.enter_context(nc.Block())

        def make_body(e):
            my = per[e]

            def body(eng):
                wm = defaultdict(int)
                dw = {}

                def wait_dma(dop):
                    key = id(dop.sem)
                    if dw.get(key, 0) < dop.val:
                        eng.wait_ge(dop.sem, dop.val)
                        dw[key] = dop.val

                for op in my:
                    for d in sorted(op.deps):
                        dop = ops[d]
                        if dop.dma:
                            wait_dma(dop)
                        elif wm[dop.engine] < dop.count:
                            eng.wait_ge(sems[dop.engine], dop.count)
                            wm[dop.engine] = dop.count
                    if op.dma and op.prev is not None:
                        wait_dma(op.prev)
                    ins = op.emit(eng)
                    if op.dma:
                        ins.then_inc(op.sem, 16)
                    elif op.needed:
                        ins.then_inc(sems[e], 1)
                if e == "sync":
                    for q in pools:
                        for k in range(self.NPOOL):
                            if dtot[q][k] > 0:
                                eng.wait_ge(pools[q][k], dtot[q][k])
            return body

        for e in self.ENGS:
            getattr(block, e)(make_body(e))


def build_program():
    nc = bass.Bass("TRN2", target_bir_lowering=False)
    P = Prog(nc)
    es = ExitStack()

    def din(name, shape, dt=F32):
        return nc.dram_tensor(name, list(shape), dt, kind="ExternalInput").ap()

    def dout(name, shape, dt=F32):
        return nc.dram_tensor(name, list(shape), dt, kind="ExternalOutput").ap()

    def sb(name, shape, dt):
        return es.enter_context(nc.sbuf_tensor(name, list(shape), dt))

    def psum(name, shape, dt):
        return es.enter_context(nc.psum_tensor(name, list(shape), dt))

    xp = din("xp", [NSEQ, T, D])
    xs = din("xs", [NS, D])
    ck = [din("ck%d" % i, [L, NS, PAIRS[i][0], 512]) for i in range(3)]
    stc = din("stc", [L, NS, 2, 2 * DFF])
    w_in = din("w_in", [L, D, 5376])
    wqkv = din("wqkv", [L, D, 6 * 384])
    wmrg = din("wmrg", [L, 8, 128, 2816])
    wo_r = din("wo_r", [L, 8, 128, 1024])
    wup_r = din("wup_r", [L, NCH, 128, 2048])
    w_dn = din("w_dn", [L, DFF, D])
    gm = din("gm", [L, 128, 8])
    gf = din("gf", [L, 128, 8])
    gv = din("gv", [L, 512])
    gqk = din("gqk", [L, 256])
    wsT = din("wsT", [L, 128, 1024])
    bT = din("bT", [L, 128, 512])
    cw = din("cw", [L, 128, 3 * 44])
    cb = din("cb", [L, 128, 44])
    w00 = din("w00", [L, 128, 4])
    b00 = din("b00", [L, 128, 4])
    c_ident = din("c_ident", [128, 128])
    c_tril = din("c_tril", [128, 128])
    c_mask = din("c_mask", [128, 512])
    c_masks = din("c_masks", [128, 4 * 8 + 8])
    c_cos = din("c_cos", [3, 128, 16 * 32])
    c_sin = din("c_sin", [3, 128, 16 * 32])
    c_cs_s = din("c_cs_s", [NS, 64])

    y = dout("y", [NSEQ, T, D])
    ys = dout("ys", [NS, D])
    kvp = [dout("kvp%d" % i, [L, NSEQ, PAIRS[i][0], 512]) for i in range(3)]
    cvp = dout("cvp", [L, NSEQ, 2, 2 * DFF])
    kvs = [dout("kvs%d" % i, [L, NS, 512]) for i in range(3)]
    cvs = dout("cvs", [L, NS, 2, 2 * DFF])
    vch = dout("vch", [L, NS, 512])

    xT = sb("xT", [128, 8, T], F32)
    xnT = sb("xnT", [128, 8, T], BF16)
    aT = sb("aT", [128, 4, T], BF16)
    bTo = sb("bTo", [128, 2, T], BF16)
    ws = sb("ws", [128, 8 * T], BF16)
    ws32 = ws[:].bitcast(F32)
    acc = ws32[:, 0:2 * T].rearrange("p (h t) -> p h t", h=2)
    qTb = ws[:, 4 * T:5 * T]
    kTb = ws[:, 5 * T:6 * T]
    Vb = ws[:, 6 * T:8 * T].rearrange("p (b h c) -> p b h c", b=16, h=2)
    mT = ws[:].rearrange("p (k t) -> p k t", k=8)
    aT32 = aT[:].rearrange("p k t -> p (k t)").bitcast(F32)
    upb = [[aT32[:, (g * 2 + i) * 514:(g * 2 + i + 1) * 514] for i in range(2)] for g in range(2)]

    xTs = sb("xTs", [128, 8, NS], F32)
    xnTs = sb("xnTs", [128, 8, NS], BF16)
    aTs = sb("aTs", [128, 4, NS], BF16)
    bTos = sb("bTos", [128, 2, NS], BF16)
    mTs = sb("mTs", [128, 8, NS], BF16)
    hTs = sb("hTs", [128, 8, NS], BF16)
    accs = sb("accs", [128, 2, NS], F32)
    qTs = sb("qTs", [128, NS], BF16)
    kTs = sb("kTs", [128, NS], BF16)
    vnew = sb("vnew", [NS, 2, 128], BF16)
    kcb = [sb("kcb%d" % i, [128, 128], BF16) for i in range(2)]
    kcT = [sb("kcT%d" % i, [128, 128], BF16) for i in range(2)]
    Vc = [sb("Vc%d" % i, [128, 2, 128], BF16) for i in range(2)]
    pTs = sb("pTs", [128, NS + 1, 8], BF16)
    upls = sb("upls", [128, 44, NS], F32)
    stTs = sb("stTs", [128, 44, 2, NS], F32)
    cs_s = sb("cs_s", [NS, 64], F32)

    NSLAB = 5
    LA = 2
    slab = [sb("slab%d" % i, [128, 2048], BF16) for i in range(NSLAB)]
    gm_sb = sb("gm_sb", [128, 8], F32)
    gf_sb = sb("gf_sb", [128, 8], F32)
    gv_sb = sb("gv_sb", [128, 512], F32)
    gqk_sb = sb("gqk_sb", [128, 256], F32)
    gpar = sb("gpar", [128, 1024], F32)
    ws_b = gpar[:, 0:512].bitcast(BF16).rearrange("p (g t) -> p g t", g=8)
    bT_sb = gpar[:, 512:1024].rearrange("p (a b) -> p a b", a=4)
    cw_sb = sb("cw_sb", [128, 3, 44], F32)
    cb_sb = sb("cb_sb", [128, 44], F32)
    w00_sb = sb("w00_sb", [128, 4], F32)
    b00_sb = sb("b00_sb", [128, 4], F32)
    ident_f = sb("ident_f", [128, 128], F32)
    ident_b = sb("ident_b", [128, 128], BF16)
    ones_b = sb("ones_b", [128, 128], BF16)
    tril_b = sb("tril_b", [128, 128], BF16)
    mask_b = sb("mask_b", [128, 512], BF16)
    masks_b = sb("masks_b", [128, 40], BF16)
    cos_sb = sb("cos_sb", [128, 16, 32], F32)
    sin_sb = sb("sin_sb", [128, 16, 32], F32)
    mhalf = sb("mhalf", [128, 1], F32)
    eps_t = sb("eps_t", [128, 1], F32)
    zeros2 = sb("zeros2", [128, 2], F32)
    upl = sb("upl", [128, 44, 2], F32)
    fz_tmp = sb("fz_tmp", [128, 128], F32)
    fz_rd = sb("fz_rd", [128, 128], F32)
    cry = [[sb("cry%d%d" % (g_, i_), [128, 2], F32) for i_ in range(2)] for g_ in range(2)]

    class Rot:
        def __init__(self, tiles):
            self.t = tiles
            self.i = 0

        def get(self):
            t = self.t[self.i % len(self.t)]
            self.i += 1
            return t

    tf = Rot([sb("tf%d" % i, [128, 512], F32) for i in range(5)])
    tb = Rot([sb("tb%d" % i, [128, 512], BF16) for i in range(4)])
    tsm = Rot([sb("tsm%d" % i, [128, 8], F32) for i in range(12)])
    xin = Rot([ws32[:, i * 1024:(i + 1) * 1024] for i in range(8)])
    tfh = Rot([t[:, h * 256:(h + 1) * 256] for t in tf.t for h in range(2)])
    tbh = Rot([t[:, h * 256:(h + 1) * 256] for t in tb.t[2:4] for h in range(2)])
    tbp = Rot(tb.t[0:2])

    psf = Rot([psum("psf%d" % i, [128, 512], F32) for i in range(6)])
    psb = Rot([psum("psb%d" % i, [128, 1024], BF16) for i in range(2)])
    PS = psf.get
    PSB = psb.get
    psq = Rot(psf.t[0:3])
    psa = Rot(psf.t[3:6])

    st = {"dry": True, "seq": 0, "issued": 0, "plan": []}

    def load_slab(make):
        j = st["seq"]
        st["seq"] += 1
        sl = slab[j % NSLAB]
        if st["dry"]:
            st["plan"].append(make(sl))
        else:
            while st["issued"] < len(st["plan"]) and st["issued"] <= j + LA:
                for dst, src in st["plan"][st["issued"]]:
                    P.dma("gpsimd", dst, src)
                st["issued"] += 1
        return sl

    def bc_row(dram2d, row, n):
        a = dram2d[row:row + 1, :]
        return bass.AP(a.tensor, a.offset, [[0, 128], [1, n]])

    def load_layer_params(l):
        P.dma("sync", gm_sb[:], gm[l])
        P.dma("sync", gf_sb[:], gf[l])
        P.dma("sync", gv_sb[:], bc_row(gv, l, 512))
        P.dma("sync", gqk_sb[:], bc_row(gqk, l, 256))
        P.dma("gpsimd", gpar[:, 0:512].bitcast(BF16), wsT[l])
        P.dma("sync", gpar[:, 512:1024], bT[l])
        P.dma("sync", cw_sb[:].rearrange("p a b -> p (a b)"), cw[l])
        P.dma("sync", cb_sb[:], cb[l])
        P.dma("sync", w00_sb[:], w00[l])
        P.dma("sync", b00_sb[:], b00[l])
        P.tt("vector", ws_b, ws_b, tril_b[:].unsqueeze(1).to_broadcast([128, 8, 128]), ALU.mult)

    class Tile:
        def __init__(self, sample, i):
            self.sample = sample
            self.i = i
            self.n = NS if sample else 512

        def sl(self, buf3, k):
            if self.sample:
                return buf3[:, k, :]
            return buf3[:, k, self.i * 512:(self.i + 1) * 512]

    def buf(tile, prompt_buf, sample_buf):
        return sample_buf if tile.sample else prompt_buf

    def stage_norm(src, dst, g_sb, tile):
        n = tile.n
        ps = PS()
        for kc in range(8):
            sq = tb.get()
            P.act(sq[:, 0:n], tile.sl(src, kc), AF.Square)
            P.mm(ps[:, 0:n], lhsT=ones_b[:], rhs=sq[:, 0:n], start=(kc == 0), stop=(kc == 7))
        t1 = tf.get()
        P.act(t1[:, 0:n], ps[:, 0:n], AF.Sqrt, scale=1.0 / D, bias=eps_t[:, 0:1])
        rstd = tf.get()
        P.recip(rstd[:, 0:n], t1[:, 0:n])
        for kc in range(8):
            P.stt(tile.sl(dst, kc), tile.sl(src, kc), g_sb[:, kc:kc + 1], rstd[:, 0:n], ALU.mult, ALU.mult)

    def rs_small(ss, m, w, inv_n):
        s2 = tsm.get()
        P.ts("vector", s2[0:m, 0:w], ss, inv_n, EPS, ALU.mult, ALU.add)
        rs = tsm.get()
        P.tt("gpsimd", rs[0:m, 0:w], s2[0:m, 0:w], mhalf[0:m, 0:1].to_broadcast([m, w]), ALU.pow)
        return rs

    def q_mm(slA, slB, xn_cols, m, pipelined=False):
        ps = psq.get() if pipelined else PS()
        for kc in range(8):
            P.mm(ps[0:m, 0:384], lhsT=xn_cols(kc), rhs=(slA[:, kc, :] if kc < 5 else slB[:, kc - 5, :]),
                 start=(kc == 0), stop=(kc == 7))
        return ps

    def q_chA(ps, m):
        sq = tfh.get()
        P.act(sq[0:m, :], ps[0:m, 0:256], AF.Square)
        ss = tsm.get()
        P.reduce(ss[0:m, 0:4], sq[0:m, :].rearrange("p (h d) -> p h d", h=4))
        rs = rs_small(ss[0:m, 0:4], m, 4, 1.0 / 64)
        return (ps, sq, rs)

    def q_chB(stA, m, cosv, sinv, vdst0, vdst1):
        ps, qn, rs = stA
        P.tt("vector", qn[0:m, :].rearrange("p (h d) -> p h d", h=4),
             ps[0:m, 0:256].rearrange("p (h d) -> p h d", h=4),
             rs[0:m, 0:4].unsqueeze(2).to_broadcast([m, 4, 64]), ALU.mult)
        P.copy("scalar", vdst0, ps[0:m, 256:320])
        P.copy("scalar", vdst1, ps[0:m, 320:384])
        stg = tfh.get()
        P.copy("scalar", stg[0:m, 128:256], ps[0:m, 256:384])
        P.tt("vector", qn[0:m, :], qn[0:m, :], gqk_sb[0:m, :], ALU.mult)
        qv = qn[0:m, :].rearrange("p (h two d) -> p h two d", h=4, two=2)
        ra = tfh.get()
        rav = ra[0:m, :].rearrange("p (h two d) -> p h two d", h=4, two=2)
        cosb = cosv.unsqueeze(1).unsqueeze(1).to_broadcast([m, 4, 2, 32])
        sinb = sinv.unsqueeze(1).to_broadcast([m, 4, 32])
        P.tt("vector", rav, qv, cosb, ALU.mult)
        rb = tfh.get()
        rbv = rb[0:m, :].rearrange("p (h two d) -> p h two d", h=4, two=2)
        P.tt("gpsimd", rbv[:, :, 1, :], qv[:, :, 0, :], sinb, ALU.mult)
        P.stt(rbv[:, :, 0, :], qv[:, :, 1, :], -1.0, sinb, ALU.mult, ALU.mult)
        return (stg, ra, rb)

    def q_chC(stB, m, kv_dst):
        stg, ra, rb = stB
        P.tt("vector", stg[0:m, 0:128], ra[0:m, 128:256], rb[0:m, 128:256], ALU.add)
        if kv_dst is not None:
            P.dma("sync", kv_dst[0], stg[0:m, 0:128])
            P.dma("sync", kv_dst[1], stg[0:m, 128:256])
        qk = tbh.get()
        P.tt("vector", qk[0:m, 0:128], ra[0:m, 0:128], rb[0:m, 0:128], ALU.add)
        P.copy("gpsimd", qk[0:m, 128:256], stg[0:m, 0:128])
        return qk

    def q_chain(ps, m, cosv, sinv, kv_dst, vdst0, vdst1):
        stA = q_chA(ps, m)
        stB = q_chB(stA, m, cosv, sinv, vdst0, vdst1)
        return q_chC(stB, m, kv_dst)

    def q_tr(qk, m, qdst, kdst):
        pt = PSB()
        P.tr(pt[:, 0:m], qk[0:m, 0:128], ident_b[0:m, 0:m])
        P.tr(pt[:, 128:128 + m], qk[0:m, 128:256], ident_b[0:m, 0:m])
        P.copy("scalar", qdst, pt[:, 0:m])
        P.copy("scalar", kdst, pt[:, 128:128 + m])

    def qkv_block(slA, slB, xn_cols, m, cosv, sinv, kv_dst, vdst0, vdst1, qdst, kdst):
        ps = q_mm(slA, slB, xn_cols, m)
        qk = q_chain(ps, m, cosv, sinv, kv_dst, vdst0, vdst1)
        q_tr(qk, m, qdst, kdst)

    def finalize_heads(accv, dstv, hp, n0, n, small=False):
        if small:
            tmp, rd = fz_tmp, fz_rd
            P.copy("vector", tmp[0:64, 0:n], accv[64:128, 0, n0:n0 + n])
            P.copy("vector", tmp[64:128, 0:n], accv[0:64, 1, n0:n0 + n])
            P.recip(rd[:, 0:n], tmp[:, 0:n])
            P.tt("vector", dstv[0:64, hp, n0:n0 + n], accv[0:64, 0, n0:n0 + n], rd[0:64, 0:n], ALU.mult)
            P.tt("vector", dstv[64:128, hp, n0:n0 + n], accv[64:128, 1, n0:n0 + n], rd[64:128, 0:n], ALU.mult)
            return
        tmp = tf.get()
        P.copy("vector", tmp[0:64, 0:n], accv[64:128, 0, n0:n0 + n])
        P.copy("vector", tmp[64:128, 0:n], accv[0:64, 1, n0:n0 + n])
        rd = tf.get()
        P.recip(rd[:, 0:n], tmp[:, 0:n])
        P.tt("vector", dstv[0:64, hp, n0:n0 + n], accv[0:64, 0, n0:n0 + n], rd[0:64, 0:n], ALU.mult)
        P.tt("vector", dstv[64:128, hp, n0:n0 + n], accv[64:128, 1, n0:n0 + n], rd[64:128, 0:n], ALU.mult)

    n_layers = DEV.get("layers", L)
    n_seq = DEV.get("seqs", NSEQ)

    def record_all():
        P.dma("sync", ident_f[:], c_ident[:, :])
        P.dma("gpsimd", ident_b[:], c_ident[:, :])
        P.dma("gpsimd", tril_b[:], c_tril[:, :])
        P.dma("gpsimd", mask_b[:], c_mask[:, :])
        P.dma("gpsimd", masks_b[:], c_masks[:, :])
        P.dma("sync", cs_s[:], c_cs_s[:, :])
        P.memset("gpsimd", ones_b[:], 1.0)
        P.memset("gpsimd", mhalf[:], -0.5)
        P.memset("gpsimd", eps_t[:], EPS)
        P.memset("gpsimd", zeros2[:], 0.0)
        P.memset("gpsimd", vnew[:], 1.0)
        for b in range(2):
            P.memset("gpsimd", Vc[b][:], 1.0)
        for s in range(n_seq):
            with_sample = (s == 0) and DEV.get("sample", True)
            record_seq(s, with_sample)

    def load_x_block(s, blk):
        xi = xin.get()
        P.dma("sync", xi, xp[s, blk * 128:(blk + 1) * 128, :])
        for half in range(2):
            ps = PS()
            for k4 in range(4):
                kc = half * 4 + k4
                P.tr(ps[:, k4 * 128:(k4 + 1) * 128], xi[:, kc * 128:(kc + 1) * 128], ident_f[:])
            P.copy("scalar" if half else "vector", xT[:, half * 4:(half + 1) * 4, blk * 128:(blk + 1) * 128],
                   ps[:].rearrange("p (k t) -> p k t", k=4))

    def store_y_block(s, blk):
        xo = xin.get()
        for half in range(2):
            ps = PS()
            for k4 in range(4):
                kc = half * 4 + k4
                P.tr(ps[:, k4 * 128:(k4 + 1) * 128], xT[:, kc, blk * 128:(blk + 1) * 128], ident_f[:])
            P.copy("scalar" if half else "vector", xo[:, half * 512:(half + 1) * 512], ps[:])
        P.dma("sync", y[s, blk * 128:(blk + 1) * 128, :], xo)

    def record_seq(s, with_sample):
        if s == 0:
            for blk in range(16):
                load_x_block(s, blk)
        if with_sample:
            xi = xin.get()
            P.dma("sync", xi[0:NS, :], xs[:, :])
            for half in range(2):
                ps = PS()
                for k4 in range(4):
                    kc = half * 4 + k4
                    P.tr(ps[:, k4 * NS:(k4 + 1) * NS], xi[0:NS, kc * 128:(kc + 1) * 128], ident_f[0:NS, 0:NS])
                P.copy("vector", xTs[:, half * 4:(half + 1) * 4, :],
                       ps[:, 0:4 * NS].rearrange("p (k t) -> p k t", k=4))
        for l in range(n_layers):
            record_layer(s, l, with_sample)
        for blk in range(16):
            store_y_block(s, blk)
            if s + 1 < n_seq:
                load_x_block(s + 1, blk)
        if with_sample:
            xo = xin.get()
            for half in range(2):
                ps = PS()
                for k4 in range(4):
                    kc = half * 4 + k4
                    P.tr(ps[0:NS, k4 * 128:(k4 + 1) * 128], xTs[:, kc, :], ident_f[:])
                P.copy("vector", xo[0:NS, half * 512:(half + 1) * 512], ps[0:NS, :])
            P.dma("sync", ys[:, :], xo[0:NS, :])

    def record_layer(s, l, with_sample):
        load_layer_params(l)
        tiles = [Tile(False, i) for i in range(4)] + ([Tile(True, 0)] if with_sample else [])

        for tile in tiles:
            stage_norm(buf(tile, xT, xTs), buf(tile, xnT, xnTs), gm_sb, tile)

        if DEV.get("gmlp", True):
            for uh in range(2):
                sl = load_slab(lambda t, uh=uh: [(t[:].rearrange("p (k c) -> p k c", k=8),
                                                  w_in[l, :, uh * 256:(uh + 1) * 256].rearrange("(k p) c -> p k c", p=128))])
                slv = sl[:].rearrange("p (k c) -> p k c", k=8)
                for o2 in range(2):
                    oc = uh * 2 + o2
                    for tile in tiles:
                        n = tile.n
                        ps = PS()
                        for kc in range(8):
                            P.mm(ps[:, 0:n], lhsT=slv[:, kc, o2 * 128:(o2 + 1) * 128],
                                 rhs=tile.sl(buf(tile, xnT, xnTs), kc), start=(kc == 0), stop=(kc == 7))
                        P.act(tile.sl(buf(tile, aT, aTs), oc), ps[:, 0:n], AF.Gelu_apprx_tanh)
            slv2 = []
            for vh in range(2):
                sl = load_slab(lambda t, vh=vh: [(t[:].rearrange("p (k c) -> p k c", k=4),
                                                  w_in[l, vh * 512:(vh + 1) * 512, 512:1024].rearrange("(k p) c -> p k c", p=128))])
                slv2.append(sl[:].rearrange("p (k c) -> p k c", k=4))

            def v_mm(xn_cols, m):
                ps = PS()
                for kc in range(8):
                    P.mm(ps[0:m, :], lhsT=xn_cols(kc), rhs=slv2[kc // 4][:, kc % 4, :],
                         start=(kc == 0), stop=(kc == 7))
                return ps

            def v_chain(ps, m):
                gvf = tf.get()
                P.act(gvf[0:m, :], ps[0:m, :], AF.Gelu_apprx_tanh)
                junk = tf.get()
                ss = tsm.get()
                P.act(junk[0:m, :], gvf[0:m, :], AF.Square, accum_out=ss[0:m, 0:1])
                rs = rs_small(ss[0:m, 0:1], m, 1, 1.0 / 512)
                return gvf, rs

            def v_branch(xn_cols, m):
                return v_chain(v_mm(xn_cols, m), m)

            def v_stage2(blk, ps):
                gvf, rs = v_chain(ps, 128)
                va = tb.get()
                P.stt(va[:], gvf[:], rs[:, 0:1], gv_sb[:], ALU.mult, ALU.mult)
                return va

            def v_stage3(blk, va):
                ps = PS()
                for g in range(8):
                    j, hf = g // 2, g % 2
                    P.mm(ps[hf * 64:(hf + 1) * 64, j * 128:(j + 1) * 128],
                         lhsT=va[:, g * 64:(g + 1) * 64], rhs=ws_b[:, g, :], start=True, stop=True)
                zb = tf.get()
                zv = zb[:].rearrange("p (j t) -> p j t", j=4)
                P.tt("vector", zv, ps[:].rearrange("p (j t) -> p j t", j=4), bT_sb, ALU.add)
                av = aT[:, :, blk * 128:(blk + 1) * 128]
                P.tt("gpsimd", av, zv, av, ALU.mult)

            vst = {}
            for step in range(16 + 2):
                if step < 16:
                    vst[step] = v_mm(lambda kc, step=step: xnT[:, kc, step * 128:(step + 1) * 128], 128)
                if 0 <= step - 1 < 16:
                    vst[step - 1] = v_stage2(step - 1, vst[step - 1])
                if 0 <= step - 2 < 16:
                    v_stage3(step - 2, vst[step - 2])
            if with_sample:
                gvf, rs = v_branch(lambda kc: xnTs[:, kc, :], NS)
                vaf = tf.get()
                P.stt(vaf[0:NS, :], gvf[0:NS, :], rs[0:NS, 0:1], gv_sb[0:NS, :], ALU.mult, ALU.mult)
                P.dma("sync", vch[l], vaf[0:NS, :])
                ps = PS()
                for j in range(4):
                    P.tr(ps[:, j * NS:(j + 1) * NS], vaf[0:NS, j * 128:(j + 1) * 128], ident_f[0:NS, 0:NS])
                zb = tf.get()
                zv = zb[:, 0:4 * NS].rearrange("p (j t) -> p j t", j=4)
                P.tt("vector", zv, ps[:, 0:4 * NS].rearrange("p (j t) -> p j t", j=4),
                     w00_sb[:].unsqueeze(2).to_broadcast([128, 4, NS]), ALU.mult)
                zb2 = tf.get()
                zv2 = zb2[:, 0:4 * NS].rearrange("p (j t) -> p j t", j=4)
                P.tt("vector", zv2, zv, b00_sb[:].unsqueeze(2).to_broadcast([128, 4, NS]), ALU.add)
                P.tt("vector", aTs[:], zv2, aTs[:], ALU.mult)

        if DEV.get("attn", True):
            glist = [(hp, i) for hp in range(2) for i in range(3)][:DEV.get("ngroups", 6)]
            tabs = [(cos_sb[:].rearrange("p a b -> p (a b)"), sin_sb[:].rearrange("p a b -> p (a b)")),
                    (gpar[:, 0:512], gpar[:, 512:1024])]
            GS = {}

            def qkv_slabs(g):
                slA = load_slab(lambda t, g=g: [(t[:, 0:1920].rearrange("p (k c) -> p k c", k=5),
                                                 wqkv[l, 0:640, g * 384:(g + 1) * 384].rearrange("(k p) c -> p k c", p=128))])
                slB = load_slab(lambda t, g=g: [(t[:, 0:1152].rearrange("p (k c) -> p k c", k=3),
                                                 wqkv[l, 640:1024, g * 384:(g + 1) * 384].rearrange("(k p) c -> p k c", p=128))])
                return (slA[:, 0:1920].rearrange("p (k c) -> p k c", k=5),
                        slB[:, 0:1152].rearrange("p (k c) -> p k c", k=3))

            def open_group(gi_):
                hp, i = glist[gi_]
                win, dil = PAIRS[i]
                slA, slB = qkv_slabs(i * 2 + hp)
                cf, sf = tabs[gi_ % 2]
                P.dma("sync", cf, c_cos[i])
                P.dma("sync", sf, c_sin[i])
                if gi_ == 0:
                    P.memset("gpsimd", Vb[:, :, 0, 64:128], 1.0)
                    P.memset("gpsimd", Vb[:, :, 1, 0:64], 1.0)
                GS[gi_] = dict(hp=hp, i=i, win=win, dil=dil, slA=slA, slB=slB,
                               cos=cf.rearrange("p (a b) -> p a b", a=16), sin=sf.rearrange("p (a b) -> p a b", a=16))

            def blk_info(G, blk):
                dil, win, hp, i = G["dil"], G["win"], G["hp"], G["i"]
                n_, r_ = blk // dil, blk % dil
                base = dil * 128 * n_ + r_
                kv_dst = None
                t0 = T - win
                if base >= t0:
                    row0 = base - t0
                    dd = kvp[i][l, s]
                    kd = bass.AP(dd.tensor, dd.offset + row0 * 512 + hp * 128, [[dil * 512, 128], [1, 128]])
                    vd = bass.AP(dd.tensor, dd.offset + row0 * 512 + 256 + hp * 128, [[dil * 512, 128], [1, 128]])
                    kv_dst = (kd, vd)
                tok = (lambda kc, base=base, dil=dil: xnT[:, kc, base:base + 127 * dil + 1:dil])
                return tok, kv_dst, base

            def st_mm(G, blk):
                tok, _, _ = blk_info(G, blk)
                return q_mm(G["slA"], G["slB"], tok, 128, pipelined=True)

            def st_A(G, blk, ps):
                return q_chA(ps, 128)

            def st_B(G, blk, stA):
                return q_chB(stA, 128, G["cos"][:, blk, :], G["sin"][:, blk, :],
                             Vb[:, blk, 0, 0:64], Vb[:, blk, 1, 64:128])

            def st_C(G, blk, stB):
                _, kv_dst, _ = blk_info(G, blk)
                return q_chC(stB, 128, kv_dst)

            def st_tr(G, blk, qk):
                q_tr(qk, 128, qTb[:, blk * 128:(blk + 1) * 128], kTb[:, blk * 128:(blk + 1) * 128])

            def key_blocks(G, blk):
                dil = G["dil"]
                return ([blk - dil] if blk // dil >= 1 else []) + [blk]

            def st_qk(G, blk):
                kbs = key_blocks(G, blk)
                nk = len(kbs)
                pT = tbp.get()
                for hh in range(2):
                    pss = psa.get()
                    if nk == 2:
                        P.mm(pss[:, 0:256], lhsT=ident_b[:], rhs=mask_b[:, 0:256], start=True, stop=False)
                    else:
                        P.mm(pss[:, 0:128], lhsT=ident_b[:], rhs=mask_b[:, 128:256], start=True, stop=False)
                    for ki, kb in enumerate(kbs):
                        P.mm(pss[:, ki * 128:(ki + 1) * 128],
                             lhsT=kTb[hh * 64:(hh + 1) * 64, kb * 128:(kb + 1) * 128],
                             rhs=qTb[hh * 64:(hh + 1) * 64, blk * 128:(blk + 1) * 128],
                             start=False, stop=(ki == nk - 1))
                    P.act(pT[:, hh * nk * 128:(hh + 1) * nk * 128], pss[:, 0:nk * 128], AF.Exp, scale=0.125)
                return pT

            def st_pv(G, blk, pT):
                kbs = key_blocks(G, blk)
                nk = len(kbs)
                dil = G["dil"]
                _, _, base = blk_info(G, blk)
                pso = psa.get()
                for hh in range(2):
                    for ki, kb in enumerate(kbs):
                        c0 = (hh * nk + ki) * 128
                        P.mm(pso[:, hh * 128:(hh + 1) * 128], lhsT=Vb[:, kb, hh, :],
                             rhs=pT[:, c0:c0 + 128], start=(ki == 0), stop=(ki == nk - 1))
                dst = acc[:, :, base:base + 127 * dil + 1:dil]
                src = pso[:, 0:256].rearrange("p (h q) -> p h q", h=2)
                if G["i"] == 0:
                    P.copy("scalar", dst, src)
                else:
                    P.tt("vector", dst, src, dst, ALU.add)

            NV = len(glist) * 16
            stv = {}
            stages = [st_mm, st_A, st_B, st_C, st_tr, st_qk, st_pv]
            for step in range(NV + 6):
                for si in (0, 3, 1, 2, 4, 5, 6):
                    fn = stages[si]
                    v = step - si
                    if not (0 <= v < NV):
                        continue
                    gi_, blk = divmod(v, 16)
                    if si == 0 and blk == 0:
                        open_group(gi_)
                    G = GS[gi_]
                    if si == 0:
                        stv[v] = fn(G, blk)
                    elif si in (4, 6):
                        fn(G, blk, stv[v])
                    elif si == 5:
                        stv[v] = fn(G, blk)
                    else:
                        stv[v] = fn(G, blk, stv[v])
                c = step - (16 * 3 + 5)
                if len(glist) == 6 and 0 <= c < 16:
                    finalize_heads(acc, bTo, 0, c * 128, 128, small=True)
            if len(glist) == 6:
                for tl in range(4):
                    finalize_heads(acc, bTo, 1, tl * 512, 512)

            if with_sample:
                for hp in range(2):
                    for i, (win, dil) in enumerate(PAIRS):
                        slA, slB = qkv_slabs(i * 2 + hp)

                        def load_cache(b, i=i, dil=dil, hp=hp):
                            kc_ = kcb[b % 2]
                            vc_ = Vc[b % 2]
                            src = ck[i][l, b]
                            P.dma("gpsimd", kc_[:], bass.AP(src.tensor, src.offset + hp * 128, [[dil * 512, 128], [1, 128]]))
                            P.dma("gpsimd", vc_[:, 0, 0:64],
                                  bass.AP(src.tensor, src.offset + 256 + hp * 128, [[dil * 512, 128], [1, 64]]))
                            P.dma("gpsimd", vc_[:, 1, 64:128],
                                  bass.AP(src.tensor, src.offset + 256 + hp * 128 + 64, [[dil * 512, 128], [1, 64]]))

                        load_cache(0)
                        load_cache(1)
                        qkv_block(slA, slB, lambda kc: xnTs[:, kc, :], NS, cs_s[0:NS, 0:32], cs_s[0:NS, 32:64],
                                  (kvs[i][l, :, hp * 128:(hp + 1) * 128], kvs[i][l, :, 256 + hp * 128:256 + (hp + 1) * 128]),
                                  vnew[0:NS, 0, 0:64], vnew[0:NS, 1, 64:128], qTs[:, :], kTs[:, :])
                        for hh in range(2):
                            pss = PS()
                            P.mm(pss[0:NS, 0:4], lhsT=ident_b[0:NS, 0:NS], rhs=masks_b[0:NS, 32 + hh * 4:36 + hh * 4],
                                 start=True, stop=False)
                            P.mm(pss[0:NS, 0:4], lhsT=kTs[hh * 64:(hh + 1) * 64, :],
                                 rhs=qTs[hh * 64:(hh + 1) * 64, :], start=False, stop=True)
                            P.act(pTs[0:NS, NS, hh * 4:(hh + 1) * 4], pss[0:NS, 0:4], AF.Exp, scale=0.125)
                        first = [i == 0]

                        def acc_add(pso):
                            srcs = pso[:, 0:8].rearrange("p (h q) -> p h q", h=2)
                            if first[0]:
                                P.copy("vector", accs[:], srcs)
                                first[0] = False
                            else:
                                P.tt("vector", accs[:], srcs, accs[:], ALU.add)

                        pso = PS()
                        for hh in range(2):
                            P.mm(pso[:, hh * 4:(hh + 1) * 4], lhsT=vnew[0:NS, hh, :], rhs=pTs[0:NS, NS, hh * 4:(hh + 1) * 4],
                                 start=True, stop=True)
                        acc_add(pso)
                        for b in range(NS):
                            kc_ = kcb[b % 2]
                            vc_ = Vc[b % 2]
                            pt = PSB()
                            P.tr(pt[:, 0:128], kc_[:], ident_b[:])
                            kT_ = kcT[b % 2]
                            P.copy("vector", kT_[:], pt[:, 0:128])
                            for hh in range(2):
                                pss = PS()
                                P.mm(pss[:, 0:4], lhsT=ident_b[:], rhs=masks_b[:, b * 8 + hh * 4:b * 8 + hh * 4 + 4],
                                     start=True, stop=False)
                                P.mm(pss[:, 0:4], lhsT=kT_[hh * 64:(hh + 1) * 64, :],
                                     rhs=qTs[hh * 64:(hh + 1) * 64, :], start=False, stop=True)
                                P.act(pTs[:, b, hh * 4:(hh + 1) * 4], pss[:, 0:4], AF.Exp, scale=0.125)
                            pso = PS()
                            for hh in range(2):
                                P.mm(pso[:, hh * 4:(hh + 1) * 4], lhsT=vc_[:, hh, :], rhs=pTs[:, b, hh * 4:(hh + 1) * 4],
                                     start=True, stop=True)
                            acc_add(pso)
                            if b + 2 < NS:
                                load_cache(b + 2)
                    finalize_heads(accs, bTos, hp, 0, NS)

        if DEV.get("merge", True):
            for jc in range(8):
                sga_ = load_slab(lambda t, jc=jc: [(t[:, 0:1024], wmrg[l, jc][:, 0:1024])])
                sgb_ = load_slab(lambda t, jc=jc: [(t[:, 0:1024], wmrg[l, jc][:, 1024:2048])])
                sab_ = load_slab(lambda t, jc=jc: [(t[:, 0:768], wmrg[l, jc][:, 2048:2816])])
                wga = sga_[:, 0:1024].rearrange("p (k c) -> p k c", k=8)
                wgb = sgb_[:, 0:1024].rearrange("p (k c) -> p k c", k=8)
                wab = sab_[:, 0:768].rearrange("p (k c) -> p k c", k=6)
                for tile in tiles:
                    n = tile.n
                    pga, pgb, pa, pb = PS(), PS(), PS(), PS()
                    xn_ = buf(tile, xnT, xnTs)
                    for kc in range(8):
                        P.mm(pga[:, 0:n], lhsT=wga[:, kc, :], rhs=tile.sl(xn_, kc), start=(kc == 0), stop=(kc == 7))
                    for kc in range(8):
                        P.mm(pgb[:, 0:n], lhsT=wgb[:, kc, :], rhs=tile.sl(xn_, kc), start=(kc == 0), stop=(kc == 7))
                    for kc in range(4):
                        P.mm(pa[:, 0:n], lhsT=wab[:, kc, :], rhs=tile.sl(buf(tile, aT, aTs), kc),
                             start=(kc == 0), stop=(kc == 3))
                    for kc in range(2):
                        P.mm(pb[:, 0:n], lhsT=wab[:, 4 + kc, :], rhs=tile.sl(buf(tile, bTo, bTos), kc),
                             start=(kc == 0), stop=(kc == 1))
                    sga, sgb = tf.get(), tf.get()
                    P.act(sga[:, 0:n], pga[:, 0:n], AF.Sigmoid)
                    P.act(sgb[:, 0:n], pgb[:, 0:n], AF.Sigmoid)
                    P.tt("vector", sga[:, 0:n], sga[:, 0:n], pa[:, 0:n], ALU.mult)
                    P.tt("vector", sgb[:, 0:n], sgb[:, 0:n], pb[:, 0:n], ALU.mult)
                    P.tt("gpsimd", tile.sl(buf(tile, mT, mTs), jc), sga[:, 0:n], sgb[:, 0:n], ALU.add)
            for oc in range(8):
                sl = load_slab(lambda t, oc=oc: [(t[:, 0:1024], wo_r[l, oc])])
                slv = sl[:, 0:1024].rearrange("p (k c) -> p k c", k=8)
                for tile in tiles:
                    n = tile.n
                    ps = PS()
                    for kc in range(8):
                        P.mm(ps[:, 0:n], lhsT=slv[:, kc, :], rhs=tile.sl(buf(tile, mT, mTs), kc),
                             start=(kc == 0), stop=(kc == 7))
                    xd = tile.sl(buf(tile, xT, xTs), oc)
                    P.tt("vector", xd, ps[:, 0:n], xd, ALU.add)

        if DEV.get("ffn", True):
            for tile in tiles:
                stage_norm(buf(tile, xT, xTs), buf(tile, xnT, xnTs), gf_sb, tile)
            if with_sample:
                P.dma("sync", cvs[l, :, 0, :], stc[l, :, 1, :])
                for q4 in range(11):
                    stg = tf.get()
                    P.dma("sync", stg[0:2 * NS, :], stc[l, :, :, q4 * 512:(q4 + 1) * 512].rearrange("b j c -> (b j) c"))
                    ps = PS()
                    for k4 in range(4):
                        P.tr(ps[:, k4 * 8:(k4 + 1) * 8], stg[0:2 * NS, k4 * 128:(k4 + 1) * 128], ident_f[0:2 * NS, 0:2 * NS])
                    for k4 in range(4):
                        P.copy("vector", stTs[:, q4 * 4 + k4, :, :],
                               ps[:, k4 * 8:(k4 + 1) * 8].rearrange("p (b j) -> p j b", j=2))
            for gi in range(2):
                P.copy("scalar", cry[gi][0][:, 0:2], zeros2[:])
            groups = [(0, 8), (8, 8), (16, 6)]
            for (c_lo, gn) in groups:
                def hsl(tile, c):
                    if tile.sample:
                        return hTs[:, c, :]
                    return mT[:, c, tile.i * 512:(tile.i + 1) * 512]

                for c in range(gn):
                    cc = c_lo + c
                    sl = load_slab(lambda t, cc=cc: [(t[:, 0:1024], wup_r[l, cc][:, 0:1024]), (t[:, 1024:2048], wup_r[l, cc][:, 1024:2048])])
                    slv = sl[:].rearrange("p (g k c) -> p g k c", g=2, k=8)
                    for tile in tiles:
                        n = tile.n
                        cres = []
                        for gi in range(2):
                            ch = gi * NCH + cc
                            ps = PS()
                            for kc in range(8):
                                P.mm(ps[:, 0:n], lhsT=slv[:, gi, kc, :], rhs=tile.sl(buf(tile, xnT, xnTs), kc),
                                     start=(kc == 0), stop=(kc == 7))
                            c0 = tf.get()
                            P.act(c0[:, 0:n], ps[:, 0:n], AF.Identity, scale=cw_sb[:, 2, ch:ch + 1],
                                  bias=cb_sb[:, ch:ch + 1])
                            if tile.sample:
                                P.copy("scalar", upls[:, ch, :], ps[:, 0:n])
                                P.stt(c0[:, 0:n], stTs[:, ch, 1, :], cw_sb[:, 1, ch:ch + 1], c0[:, 0:n], ALU.mult, ALU.add)
                                P.stt(c0[:, 0:n], stTs[:, ch, 0, :], cw_sb[:, 0, ch:ch + 1], c0[:, 0:n], ALU.mult, ALU.add)
                            else:
                                cr = cry[gi][tile.i % 2]
                                nxt = cry[gi][(tile.i + 1) % 2]
                                w1c, w0c = cw_sb[:, 1, ch:ch + 1], cw_sb[:, 0, ch:ch + 1]
                                if tile.i == 3:
                                    P.copy("scalar", upl[:, ch, :], ps[:, 510:512])
                                    P.copy("scalar", nxt[:, 0:2], zeros2[:])
                                else:
                                    P.act(nxt[:, 0:2], ps[:, 510:512], AF.Copy, scale=w0c)
                                    P.act(nxt[:, 0:1], ps[:, 511:512], AF.Identity, scale=w1c, bias=nxt[:, 0:1])
                                P.stt(c0[:, 1:512], ps[:, 0:511], w1c, c0[:, 1:512], ALU.mult, ALU.add)
                                P.stt(c0[:, 2:512], ps[:, 0:510], w0c, c0[:, 2:512], ALU.mult, ALU.add)
                                P.tt("vector", c0[:, 0:2], c0[:, 0:2], cr[:, 0:2], ALU.add)
                            cres.append(c0)
                        sg = tb.get()
                        P.act(sg[:, 0:n], cres[0][:, 0:n], AF.Silu)
                        P.tt("gpsimd", hsl(tile, c), sg[:, 0:n], cres[1][:, 0:n], ALU.mult)
                for oc in range(8):
                    def mk(t, oc=oc, c_lo=c_lo, gn=gn):
                        a = w_dn[l]
                        src = bass.AP(a.tensor, a.offset + c_lo * 128 * D + oc * 128, [[D, 128], [128 * D, gn], [1, 128]])
                        return [(t[:, 0:gn * 128].rearrange("p (c k) -> p c k", c=gn), src)]
                    sl = load_slab(mk)
                    slv = sl[:, 0:gn * 128].rearrange("p (c k) -> p c k", c=gn)
                    for tile in tiles:
                        n = tile.n
                        ps = PS()
                        for c in range(gn):
                            P.mm(ps[:, 0:n], lhsT=slv[:, c, :], rhs=hsl(tile, c), start=(c == 0), stop=(c == gn - 1))
                        xd = tile.sl(buf(tile, xT, xTs), oc)
                        P.tt("vector", xd, ps[:, 0:n], xd, ALU.add)
            for q4 in range(11):
                ps = PS()
                for k4 in range(4):
                    P.tr(ps[0:2, k4 * 128:(k4 + 1) * 128], upl[:, q4 * 4 + k4, :], ident_f[:])
                stg = tf.get()
                P.copy("vector", stg[0:2, :], ps[0:2, :])
                P.dma("sync", cvp[l, s, :, q4 * 512:(q4 + 1) * 512], stg[0:2, :])
                if with_sample:
                    ps = PS()
                    for k4 in range(4):
                        P.tr(ps[0:NS, k4 * 128:(k4 + 1) * 128], upls[:, q4 * 4 + k4, :], ident_f[:])
                    stg = tf.get()
                    P.copy("vector", stg[0:NS, :], ps[0:NS, :])
                    P.dma("sync", cvs[l, :, 1, q4 * 512:(q4 + 1) * 512], stg[0:NS, :])

    P.dry = True
    st["dry"] = True
    record_all()
    P.dry = False
    st["dry"] = False
    st["seq"] = 0
    record_all()

    P.emit_all(es)
    es.close()
    return nc


def _host_consts():
    ident = np.eye(128, dtype=np.float32)
    tril = (np.arange(128)[:, None] <= np.arange(128)[None, :]).astype(np.float32)
    k = np.arange(128)[:, None]
    q = np.arange(128)[None, :]
    prev = np.where(k >= q, 0.0, NEG).astype(np.float32)
    cur = np.where(k <= q, 0.0, NEG).astype(np.float32)
    one = np.concatenate([prev, cur], axis=1)
    mask = np.concatenate([one, one], axis=1)
    masks = np.full((128, 40), NEG, np.float32)
    for b in range(NS):
        for hh in range(2):
            masks[:, b * 8 + hh * 4 + b] = 0.0
    for hh in range(2):
        for b in range(NS):
            masks[b, 32 + hh * 4 + b] = 0.0
    half = 32
    inv = (10000.0 ** (-np.arange(half, dtype=np.float32) / half)).astype(np.float32)
    cos = np.zeros((3, 128, 16, 32), np.float32)
    sin = np.zeros((3, 128, 16, 32), np.float32)
    for i, (win, dil) in enumerate(PAIRS):
        for blk in range(16):
            n_, r_ = blk // dil, blk % dil
            pos = (dil * 128 * n_ + r_ + dil * np.arange(128)).astype(np.float32)
            ang = pos[:, None] * inv[None, :]
            c, s_ = np.cos(ang), np.sin(ang)
            cos[i, :, blk, :] = c
            sin[i, :, blk, :] = s_
    ang = np.float32(PAST) * inv
    cs = np.concatenate([np.cos(ang), np.sin(ang)]).astype(np.float32)
    cs_s = np.tile(cs[None, :], (NS, 1))
    return dict(c_ident=ident, c_tril=tril, c_mask=mask, c_masks=masks,
                c_cos=cos.reshape(3, 128, 512), c_sin=sin.reshape(3, 128, 512), c_cs_s=cs_s)


def _host_weights(inp):
    f = lambda a: np.ascontiguousarray(np.asarray(a, dtype=np.float32))
    w_in = f(inp["w_in"])
    o_q, o_k, o_v, o_ga, o_gb = 1024, 1792, 2560, 3328, 4352
    cols = []
    for g in range(6):
        i, hp = g // 2, g % 2
        h0 = (i * 4 + hp * 2) * 64
        for o in (o_q, o_k, o_v):
            cols.append(np.arange(o + h0, o + h0 + 128))
    cols = np.concatenate(cols)
    wqkv = f(w_in[:, :, cols])
    wa, wb = f(inp["w_a_proj"]), f(inp["w_b_proj"])
    wmrg = np.zeros((L, 8, 128, 22, 128), np.float32)
    for jc in range(8):
        cs_ = slice(jc * 128, (jc + 1) * 128)
        ga = w_in[:, :, o_ga:o_gb][:, :, cs_].reshape(L, 8, 128, 128)
        gb = w_in[:, :, o_gb:][:, :, cs_].reshape(L, 8, 128, 128)
        wmrg[:, jc, :, 0:8] = ga.transpose(0, 2, 1, 3)
        wmrg[:, jc, :, 8:16] = gb.transpose(0, 2, 1, 3)
        wmrg[:, jc, :, 16:20] = wa[:, :, cs_].reshape(L, 4, 128, 128).transpose(0, 2, 1, 3)
        wmrg[:, jc, :, 20:22] = wb[:, :, cs_].reshape(L, 2, 128, 128).transpose(0, 2, 1, 3)
    wmrg = wmrg.reshape(L, 8, 128, 2816)
    wo = f(inp["w_o"]).reshape(L, 8, 128, 8, 128)
    wo_r = f(wo.transpose(0, 3, 2, 1, 4)).reshape(L, 8, 128, 1024)
    wup = f(inp["w_up"]).reshape(L, 8, 128, 2, NCH, 128)
    wup_r = f(wup.transpose(0, 4, 2, 3, 1, 5)).reshape(L, NCH, 128, 2048)
    gm = f(f(inp["g_mix"]).reshape(L, 8, 128).transpose(0, 2, 1))
    gf = f(f(inp["g_ffn"]).reshape(L, 8, 128).transpose(0, 2, 1))
    gqk = np.concatenate([f(inp["g_q"])] * 2 + [f(inp["g_k"])] * 2, axis=1)
    ws_ = f(inp["w_s"])
    wsT = f(ws_.transpose(0, 3, 1, 2)).reshape(L, 128, 1024)
    bs = f(inp["b_s"])
    bT = f(np.repeat(bs.reshape(L, 4, 2, 1, 128), 64, axis=3).reshape(L, 4, 128, 128).transpose(0, 2, 1, 3))
    bT = bT.reshape(L, 128, 512)
    cwv = f(inp["conv_w"]).reshape(L, 3, 44, 128)
    cw = f(cwv.transpose(0, 3, 1, 2)).reshape(L, 128, 132)
    cb = f(f(inp["conv_b"]).reshape(L, 44, 128).transpose(0, 2, 1))
    w00 = f(np.repeat(ws_[:, :, 0, 0].reshape(L, 4, 2, 1), 64, axis=3).reshape(L, 4, 128).transpose(0, 2, 1))
    b00 = f(np.repeat(bs[:, :, 0].reshape(L, 4, 2, 1), 64, axis=3).reshape(L, 4, 128).transpose(0, 2, 1))
    return dict(w_in=w_in, wqkv=wqkv, wmrg=wmrg, wo_r=wo_r, wup_r=wup_r, w_dn=f(inp["w_down"]),
                gm=gm, gf=gf, gv=f(inp["g_v"]), gqk=f(gqk), wsT=wsT, bT=bT, cw=cw, cb=cb, w00=w00, b00=b00)


_NC_CACHE = {}


def kernel(**inp):
    shared = _host_weights(inp)
    shared.update(_host_consts())
    xp = np.asarray(inp["x_prompt"], np.float32)
    xs = np.asarray(inp["x_sample"], np.float32)
    cks = [np.asarray(inp[k], np.float32) for k in ("cache_kv_w128", "cache_kv_w512", "cache_kv_w2048")]
    stc = np.asarray(inp["state_conv"], np.float32)
    in_maps = []
    for c in range(NCORES):
        m = dict(shared)
        m["xp"] = np.ascontiguousarray(xp[2 * c:2 * c + 2])
        m["xs"] = np.ascontiguousarray(xs[4 * c:4 * c + 4, 0])
        for i in range(3):
            a = cks[i][:, 4 * c:4 * c + 4]
            m["ck%d" % i] = np.ascontiguousarray(a.reshape(L, NS, a.shape[2], 512))
        m["stc"] = np.ascontiguousarray(stc[:, 4 * c:4 * c + 4])
        in_maps.append(m)
    key = tuple(sorted(DEV.items()))
    if key not in _NC_CACHE:
        _NC_CACHE[key] = build_program()
    nc = _NC_CACHE[key]
    res = run_bass_kernel_spmd(nc, in_maps, core_ids=list(range(NCORES)))
    R = res.results
    cat = lambda name, ax: np.concatenate([np.asarray(r[name]) for r in R], axis=ax)
    y = cat("y", 0)
    ys = cat("ys", 0).reshape(32, 1, D)
    outs = [y, ys]
    for i in range(3):
        a = cat("kvp%d" % i, 1)
        outs.append(a.reshape(L, 16, PAIRS[i][0], 2, 4, 64))
    outs.append(cat("cvp", 1))
    for i in range(3):
        outs.append(cat("kvs%d" % i, 1).reshape(L, 32, 1, 2, 4, 64))
    outs.append(cat("cvs", 1))
    outs.append(cat("vch", 1).reshape(L, 32, 1, 512))
    return tuple(np.ascontiguousarray(o.astype(np.float32)) for o in outs)
```

```python
import math
from collections import defaultdict
from contextlib import ExitStack

import numpy as np
import concourse.bass as bass
import concourse.mybir as mybir
from concourse.bass_utils import run_bass_kernel_spmd

F32 = mybir.dt.float32
BF16 = mybir.dt.bfloat16
AF = mybir.ActivationFunctionType
ALU = mybir.AluOpType
AX = mybir.AxisListType

D = 1024
T = 2048
L = 2
NSEQ = 2
NS = 4
DFF = 2816
NCH = 22
EPS = 1e-6
NEG = -30000.0
PAIRS = ((128, 1), (512, 4), (2048, 16))
PAST = 16384
NCORES = 8

DEV = {}


def _esz(dt):
    return mybir.dt.size(dt)


def _region(ap):
    t = ap.tensor
    if type(t).__name__.startswith("DRam"):
        return None
    shape = t.shape
    pstep = 1
    for s in shape[1:]:
        pstep *= s
    esz = _esz(ap.dtype)
    off = ap.offset
    p0 = off // pstep
    f0 = off % pstep
    dims = ap.ap
    npart = dims[0][1] if dims[0][0] != 0 else 1
    ext = 1
    for s, c in dims[1:]:
        ext += (c - 1) * abs(s)
    return (type(t).__name__[0] + t.name, p0, p0 + npart, f0 * esz, (f0 + ext) * esz)


class Op:
    __slots__ = ("idx", "engine", "emit", "deps", "dma", "needed", "count", "sem", "val", "prev")

    def __init__(self, idx, engine, emit, dma):
        self.idx = idx
        self.engine = engine
        self.emit = emit
        self.deps = set()
        self.dma = dma
        self.needed = False
        self.count = 0
        self.sem = None
        self.val = 0
        self.prev = None


class Prog:
    ENGS = ("tensor", "scalar", "vector", "gpsimd", "sync")
    NPOOL = 12

    def __init__(self, nc):
        self.nc = nc
        self.ops = []
        self.recs = defaultdict(list)
        self.dry = False

    def add(self, engine, emit, reads=(), writes=(), dma=False):
        if self.dry:
            return None
        op = Op(len(self.ops), engine, emit, dma)
        self.ops.append(op)
        for ap in reads:
            self._access(op, ap, False)
        for ap in writes:
            self._access(op, ap, True)
        return op

    def _access(self, op, ap, is_write):
        reg = _region(ap)
        if reg is None:
            return
        name, p0, p1, lo, hi = reg
        if name[0] == "P":
            p0, p1, lo, hi, is_write = 0, 128, 0, 2048, True
        lst = self.recs[name]
        out = []
        for r in lst:
            if r[1] <= p0 or p1 <= r[0] or r[3] <= lo or hi <= r[2]:
                out.append(r)
                continue
            dop = self.ops[r[4]]
            if dop.idx != op.idx:
                if is_write:
                    same = (dop.engine == op.engine) and not dop.dma and not op.dma and op.engine != "gpsimd"
                    if not same:
                        op.deps.add(dop.idx)
                elif r[5]:
                    op.deps.add(dop.idx)
            covered = r[0] >= p0 and r[1] <= p1 and r[2] >= lo and r[3] <= hi
            if is_write and covered:
                continue
            if (not is_write) and (not r[5]) and covered and dop.engine == op.engine and not dop.dma and not op.dma:
                continue
            out.append(r)
        out.append((p0, p1, lo, hi, op.idx, is_write))
        self.recs[name] = out

    def mm(self, out, lhsT, rhs, start=True, stop=True):
        return self.add("tensor", lambda e: e.matmul(out, lhsT=lhsT, rhs=rhs, start=start, stop=stop),
                        reads=(lhsT, rhs), writes=(out,))

    def tr(self, out, in_, ident):
        return self.add("tensor", lambda e: e.transpose(out=out, in_=in_, identity=ident),
                        reads=(in_, ident), writes=(out,))

    def act(self, out, in_, func, scale=None, bias=None, accum_out=None):
        kw = {}
        reads = [in_]
        writes = [out]
        if scale is not None:
            kw["scale"] = scale
            if not isinstance(scale, (int, float)):
                reads.append(scale)
        if bias is not None:
            kw["bias"] = bias
            if not isinstance(bias, (int, float)):
                reads.append(bias)
        if accum_out is not None:
            kw["accum_out"] = accum_out
            writes.append(accum_out)
        return self.add("scalar", lambda e: e.activation(out=out, in_=in_, func=func, **kw), reads, writes)

    def tt(self, eng, out, in0, in1, op):
        return self.add(eng, lambda e: e.tensor_tensor(out=out, in0=in0, in1=in1, op=op), (in0, in1), (out,))

    def ts(self, eng, out, in0, s1, s2, op0, op1=None):
        reads = [in0]
        for s in (s1, s2):
            if s is not None and not isinstance(s, (int, float)):
                reads.append(s)
        if op1 is None:
            return self.add(eng, lambda e: e.tensor_scalar(out=out, in0=in0, scalar1=s1, scalar2=None, op0=op0),
                            reads, (out,))
        return self.add(eng, lambda e: e.tensor_scalar(out=out, in0=in0, scalar1=s1, scalar2=s2, op0=op0, op1=op1),
                        reads, (out,))

    def stt(self, out, in0, scalar, in1, op0, op1):
        reads = [in0, in1]
        if not isinstance(scalar, (int, float)):
            reads.append(scalar)
        return self.add("vector", lambda e: e.scalar_tensor_tensor(out=out, in0=in0, scalar=scalar, in1=in1,
                                                                    op0=op0, op1=op1), reads, (out,))

    def copy(self, eng, out, in_):
        if eng == "scalar":
            return self.act(out, in_, AF.Copy)
        return self.add(eng, lambda e: e.tensor_copy(out=out, in_=in_), (in_,), (out,))

    def reduce(self, out, in_, op=ALU.add):
        return self.add("vector", lambda e: e.tensor_reduce(out=out, in_=in_, axis=AX.X, op=op), (in_,), (out,))

    def recip(self, out, in_):
        return self.add("vector", lambda e: e.reciprocal(out=out, in_=in_), (in_,), (out,))

    def memset(self, eng, out, val):
        return self.add(eng, lambda e: e.memset(out, val), (), (out,))

    def dma(self, q, out, in_):
        return self.add(q, lambda e: e.dma_start(out=out, in_=in_), (in_,), (out,), dma=True)

    def emit_all(self, es):
        nc = self.nc
        ops = self.ops
        sems = {e: es.enter_context(nc.semaphore("s_" + e)) for e in self.ENGS}
        pools = {q: [es.enter_context(nc.semaphore("d_%s_%d" % (q, i))) for i in range(self.NPOOL)]
                 for q in ("sync", "gpsimd")}
        for op in ops:
            for d in op.deps:
                ops[d].needed = True
        cnt = defaultdict(int)
        didx = defaultdict(int)
        dtot = {q: [0] * self.NPOOL for q in pools}
        dlast = {q: [None] * self.NPOOL for q in pools}
        for op in ops:
            if op.dma:
                q = op.engine
                k = didx[q] % self.NPOOL
                didx[q] += 1
                op.sem = pools[q][k]
                dtot[q][k] += 16
                op.val = dtot[q][k]
                op.prev = dlast[q][k]
                dlast[q][k] = op
            elif op.needed:
                cnt[op.engine] += 1
                op.count = cnt[op.engine]
        per = {e: [op for op in ops if op.engine == e] for e in self.ENGS}
        block = es.enter_context(nc.Block())

        def make_body(e):
            my = per[e]

            def body(eng):
                wm = defaultdict(int)
                dw = {}

                def wait_dma(dop):
                    key = id(dop.sem)
                    if dw.get(key, 0) < dop.val:
                        eng.wait_ge(dop.sem, dop.val)
                        dw[key] = dop.val

                for op in my:
                    for d in sorted(op.deps):
                        dop = ops[d]
                        if dop.dma:
                            wait_dma(dop)
                        elif wm[dop.engine] < dop.count:
                            eng.wait_ge(sems[dop.engine], dop.count)
                            wm[dop.engine] = dop.count
                    if op.dma and op.prev is not None:
                        wait_dma(op.prev)
                    ins = op.emit(eng)
                    if op.dma:
                        ins.then_inc(op.sem, 16)
                    elif op.needed:
                        ins.then_inc(sems[e], 1)
                if e == "sync":
                    for q in pools:
                        for k in range(self.NPOOL):
                            if dtot[q][k] > 0:
                                eng.wait_ge(pools[q][k], dtot[q][k])
            return body

        for e in self.ENGS:
            getattr(block, e)(make_body(e))


def build_program():
    nc = bass.Bass("TRN2", target_bir_lowering=False)
    P = Prog(nc)
    es = ExitStack()

    def din(name, shape, dt=F32):
        return nc.dram_tensor(name, list(shape), dt, kind="ExternalInput").ap()

    def dout(name, shape, dt=F32):
        return nc.dram_tensor(name, list(shape), dt, kind="ExternalOutput").ap()

    def sb(name, shape, dt):
        return es.enter_context(nc.sbuf_tensor(name, list(shape), dt))

    def psum(name, shape, dt):
        return es.enter_context(nc.psum_tensor(name, list(shape), dt))

    xp = din("xp", [NSEQ, T, D])
    xs = din("xs", [NS, D])
    ck = [din("ck%d" % i, [L, NS, PAIRS[i][0], 512]) for i in range(3)]
    stc = din("stc", [L, NS, 2, 2 * DFF])
    w_in = din("w_in", [L, D, 5376])
    wqkv = din("wqkv", [L, D, 6 * 384])
    wmrg = din("wmrg", [L, 8, 128, 2816])
    wo_r = din("wo_r", [L, 8, 128, 1024])
    wup_r = din("wup_r", [L, NCH, 128, 2048])
    w_dn = din("w_dn", [L, DFF, D])
    gm = din("gm", [L, 128, 8])
    gf = din("gf", [L, 128, 8])
    gv = din("gv", [L, 512])
    gqk = din("gqk", [L, 256])
    wsT = din("wsT", [L, 128, 1024])
    bT = din("bT", [L, 128, 512])
    cw = din("cw", [L, 128, 3 * 44])
    cb = din("cb", [L, 128, 44])
    w00 = din("w00", [L, 128, 4])
    b00 = din("b00", [L, 128, 4])
    c_ident = din("c_ident", [128, 128])
    c_tril = din("c_tril", [128, 128])
    c_mask = din("c_mask", [128, 512])
    c_masks = din("c_masks", [128, 4 * 8 + 8])
    c_cos = din("c_cos", [3, 128, 16 * 32])
    c_sin = din("c_sin", [3, 128, 16 * 32])
    c_cs_s = din("c_cs_s", [NS, 64])

    y = dout("y", [NSEQ, T, D])
    ys = dout("ys", [NS, D])
    kvp = [dout("kvp%d" % i, [L, NSEQ, PAIRS[i][0], 512]) for i in range(3)]
    cvp = dout("cvp", [L, NSEQ, 2, 2 * DFF])
    kvs = [dout("kvs%d" % i, [L, NS, 512]) for i in range(3)]
    cvs = dout("cvs", [L, NS, 2, 2 * DFF])
    vch = dout("vch", [L, NS, 512])

    xT = sb("xT", [128, 8, T], F32)
    xnT = sb("xnT", [128, 8, T], BF16)
    aT = sb("aT", [128, 4, T], BF16)
    bTo = sb("bTo", [128, 2, T], BF16)
    ws = sb("ws", [128, 8 * T], BF16)
    ws32 = ws[:].bitcast(F32)
    acc = ws32[:, 0:2 * T].rearrange("p (h t) -> p h t", h=2)
    qTb = ws[:, 4 * T:5 * T]
    kTb = ws[:, 5 * T:6 * T]
    Vb = ws[:, 6 * T:8 * T].rearrange("p (b h c) -> p b h c", b=16, h=2)
    mT = ws[:].rearrange("p (k t) -> p k t", k=8)
    aT32 = aT[:].rearrange("p k t -> p (k t)").bitcast(F32)
    upb = [[aT32[:, (g * 2 + i) * 514:(g * 2 + i + 1) * 514] for i in range(2)] for g in range(2)]

    xTs = sb("xTs", [128, 8, NS], F32)
    xnTs = sb("xnTs", [128, 8, NS], BF16)
    aTs = sb("aTs", [128, 4, NS], BF16)
    bTos = sb("bTos", [128, 2, NS], BF16)
    mTs = sb("mTs", [128, 8, NS], BF16)
    hTs = sb("hTs", [128, 8, NS], BF16)
    accs = sb("accs", [128, 2, NS], F32)
    qTs = sb("qTs", [128, NS], BF16)
    kTs = sb("kTs", [128, NS], BF16)
    vnew = sb("vnew", [NS, 2, 128], BF16)
    kcb = [sb("kcb%d" % i, [128, 128], BF16) for i in range(2)]
    kcT = [sb("kcT%d" % i, [128, 128], BF16) for i in range(2)]
    Vc = [sb("Vc%d" % i, [128, 2, 128], BF16) for i in range(2)]
    pTs = sb("pTs", [128, NS + 1, 8], BF16)
    upls = sb("upls", [128, 44, NS], F32)
    stTs = sb("stTs", [128, 44, 2, NS], F32)
    cs_s = sb("cs_s", [NS, 64], F32)

    NSLAB = 5
    LA = 2
    slab = [sb("slab%d" % i, [128, 2048], BF16) for i in range(NSLAB)]
    gm_sb = sb("gm_sb", [128, 8], F32)
    gf_sb = sb("gf_sb", [128, 8], F32)
    gv_sb = sb("gv_sb", [128, 512], F32)
    gqk_sb = sb("gqk_sb", [128, 256], F32)
    gpar = sb("gpar", [128, 1024], F32)
    ws_b = gpar[:, 0:512].bitcast(BF16).rearrange("p (g t) -> p g t", g=8)
    bT_sb = gpar[:, 512:1024].rearrange("p (a b) -> p a b", a=4)
    cw_sb = sb("cw_sb", [128, 3, 44], F32)
    cb_sb = sb("cb_sb", [128, 44], F32)
    w00_sb = sb("w00_sb", [128, 4], F32)
    b00_sb = sb("b00_sb", [128, 4], F32)
    ident_f = sb("ident_f", [128, 128], F32)
    ident_b = sb("ident_b", [128, 128], BF16)
    ones_b = sb("ones_b", [128, 128], BF16)
    tril_b = sb("tril_b", [128, 128], BF16)
    mask_b = sb("mask_b", [128, 512], BF16)
    masks_b = sb("masks_b", [128, 40], BF16)
    cos_sb = sb("cos_sb", [128, 16, 32], F32)
    sin_sb = sb("sin_sb", [128, 16, 32], F32)
    mhalf = sb("mhalf", [128, 1], F32)
    eps_t = sb("eps_t", [128, 1], F32)
    zeros2 = sb("zeros2", [128, 2], F32)
    upl = sb("upl", [128, 44, 2], F32)
    fz_tmp = sb("fz_tmp", [128, 128], F32)
    fz_rd = sb("fz_rd", [128, 128], F32)
    cry = [[sb("cry%d%d" % (g_, i_), [128, 2], F32) for i_ in range(2)] for g_ in range(2)]

    class Rot:
        def __init__(self, tiles):
            self.t = tiles
            self.i = 0

        def get(self):
            t = self.t[self.i % len(self.t)]
            self.i += 1
            return t

    tf = Rot([sb("tf%d" % i, [128, 512], F32) for i in range(5)])
    tb = Rot([sb("tb%d" % i, [128, 512], BF16) for i in range(4)])
    tsm = Rot([sb("tsm%d" % i, [128, 8], F32) for i in range(12)])
    xin = Rot([ws32[:, i * 1024:(i + 1) * 1024] for i in range(8)])
    tfh = Rot([t[:, h * 256:(h + 1) * 256] for t in tf.t for h in range(2)])
    tbh = Rot([t[:, h * 256:(h + 1) * 256] for t in tb.t[2:4] for h in range(2)])
    tbp = Rot(tb.t[0:2])

    psf = Rot([psum("psf%d" % i, [128, 512], F32) for i in range(6)])
    psb = Rot([psum("psb%d" % i, [128, 1024], BF16) for i in range(2)])
    PS = psf.get
    PSB = psb.get
    psq = Rot(psf.t[0:3])
    psa = Rot(psf.t[3:6])

    st = {"dry": True, "seq": 0, "issued": 0, "plan": []}

    def load_slab(make):
        j = st["seq"]
        st["seq"] += 1
        sl = slab[j % NSLAB]
        if st["dry"]:
            st["plan"].append(make(sl))
        else:
            while st["issued"] < len(st["plan"]) and st["issued"] <= j + LA:
                for dst, src in st["plan"][st["issued"]]:
                    P.dma("gpsimd", dst, src)
                st["issued"] += 1
        return sl

    def bc_row(dram2d, row, n):
        a = dram2d[row:row + 1, :]
        return bass.AP(a.tensor, a.offset, [[0, 128], [1, n]])

    def load_layer_params(l):
        P.dma("sync", gm_sb[:], gm[l])
        P.dma("sync", gf_sb[:], gf[l])
        P.dma("sync", gv_sb[:], bc_row(gv, l, 512))
        P.dma("sync", gqk_sb[:], bc_row(gqk, l, 256))
        P.dma("gpsimd", gpar[:, 0:512].bitcast(BF16), wsT[l])
        P.dma("sync", gpar[:, 512:1024], bT[l])
        P.dma("sync", cw_sb[:].rearrange("p a b -> p (a b)"), cw[l])
        P.dma("sync", cb_sb[:], cb[l])
        P.dma("sync", w00_sb[:], w00[l])
        P.dma("sync", b00_sb[:], b00[l])
        P.tt("vector", ws_b, ws_b, tril_b[:].unsqueeze(1).to_broadcast([128, 8, 128]), ALU.mult)

    class Tile:
        def __init__(self, sample, i):
            self.sample = sample
            self.i = i
            self.n = NS if sample else 512

        def sl(self, buf3, k):
            if self.sample:
                return buf3[:, k, :]
            return buf3[:, k, self.i * 512:(self.i + 1) * 512]

    def buf(tile, prompt_buf, sample_buf):
        return sample_buf if tile.sample else prompt_buf

    def stage_norm(src, dst, g_sb, tile):
        n = tile.n
        ps = PS()
        for kc in range(8):
            sq = tb.get()
            P.act(sq[:, 0:n], tile.sl(src, kc), AF.Square)
            P.mm(ps[:, 0:n], lhsT=ones_b[:], rhs=sq[:, 0:n], start=(kc == 0), stop=(kc == 7))
        t1 = tf.get()
        P.act(t1[:, 0:n], ps[:, 0:n], AF.Sqrt, scale=1.0 / D, bias=eps_t[:, 0:1])
        rstd = tf.get()
        P.recip(rstd[:, 0:n], t1[:, 0:n])
        for kc in range(8):
            P.stt(tile.sl(dst, kc), tile.sl(src, kc), g_sb[:, kc:kc + 1], rstd[:, 0:n], ALU.mult, ALU.mult)

    def rs_small(ss, m, w, inv_n):
        s2 = tsm.get()
        P.ts("vector", s2[0:m, 0:w], ss, inv_n, EPS, ALU.mult, ALU.add)
        rs = tsm.get()
        P.tt("gpsimd", rs[0:m, 0:w], s2[0:m, 0:w], mhalf[0:m, 0:1].to_broadcast([m, w]), ALU.pow)
        return rs

    def q_mm(slA, slB, xn_cols, m, pipelined=False):
        ps = psq.get() if pipelined else PS()
        for kc in range(8):
            P.mm(ps[0:m, 0:384], lhsT=xn_cols(kc), rhs=(slA[:, kc, :] if kc < 5 else slB[:, kc - 5, :]),
                 start=(kc == 0), stop=(kc == 7))
        return ps

    def q_chA(ps, m):
        sq = tfh.get()
        P.act(sq[0:m, :], ps[0:m, 0:256], AF.Square)
        ss = tsm.get()
        P.reduce(ss[0:m, 0:4], sq[0:m, :].rearrange("p (h d) -> p h d", h=4))
        rs = rs_small(ss[0:m, 0:4], m, 4, 1.0 / 64)
        return (ps, sq, rs)

    def q_chB(stA, m, cosv, sinv, vdst0, vdst1):
        ps, qn, rs = stA
        P.tt("vector", qn[0:m, :].rearrange("p (h d) -> p h d", h=4),
             ps[0:m, 0:256].rearrange("p (h d) -> p h d", h=4),
             rs[0:m, 0:4].unsqueeze(2).to_broadcast([m, 4, 64]), ALU.mult)
        P.copy("scalar", vdst0, ps[0:m, 256:320])
        P.copy("scalar", vdst1, ps[0:m, 320:384])
        stg = tfh.get()
        P.copy("scalar", stg[0:m, 128:256], ps[0:m, 256:384])
        P.tt("vector", qn[0:m, :], qn[0:m, :], gqk_sb[0:m, :], ALU.mult)
        qv = qn[0:m, :].rearrange("p (h two d) -> p h two d", h=4, two=2)
        ra = tfh.get()
        rav = ra[0:m, :].rearrange("p (h two d) -> p h two d", h=4, two=2)
        cosb = cosv.unsqueeze(1).unsqueeze(1).to_broadcast([m, 4, 2, 32])
        sinb = sinv.unsqueeze(1).to_broadcast([m, 4, 32])
        P.tt("vector", rav, qv, cosb, ALU.mult)
        rb = tfh.get()
        rbv = rb[0:m, :].rearrange("p (h two d) -> p h two d", h=4, two=2)
        P.tt("gpsimd", rbv[:, :, 1, :], qv[:, :, 0, :], sinb, ALU.mult)
        P.stt(rbv[:, :, 0, :], qv[:, :, 1, :], -1.0, sinb, ALU.mult, ALU.mult)
        return (stg, ra, rb)

    def q_chC(stB, m, kv_dst):
        stg, ra, rb = stB
        P.tt("vector", stg[0:m, 0:128], ra[0:m, 128:256], rb[0:m, 128:256], ALU.add)
        if kv_dst is not None:
            P.dma("sync", kv_dst[0], stg[0:m, 0:128])
            P.dma("sync", kv_dst[1], stg[0:m, 128:256])
        qk = tbh.get()
        P.tt("vector", qk[0:m, 0:128], ra[0:m, 0:128], rb[0:m, 0:128], ALU.add)
        P.copy("gpsimd", qk[0:m, 128:256], stg[0:m, 0:128])
        return qk

    def q_chain(ps, m, cosv, sinv, kv_dst, vdst0, vdst1):
        stA = q_chA(ps, m)
        stB = q_chB(stA, m, cosv, sinv, vdst0, vdst1)
        return q_chC(stB, m, kv_dst)

    def q_tr(qk, m, qdst, kdst):
        pt = PSB()
        P.tr(pt[:, 0:m], qk[0:m, 0:128], ident_b[0:m, 0:m])
        P.tr(pt[:, 128:128 + m], qk[0:m, 128:256], ident_b[0:m, 0:m])
        P.copy("scalar", qdst, pt[:, 0:m])
        P.copy("scalar", kdst, pt[:, 128:128 + m])

    def qkv_block(slA, slB, xn_cols, m, cosv, sinv, kv_dst, vdst0, vdst1, qdst, kdst):
        ps = q_mm(slA, slB, xn_cols, m)
        qk = q_chain(ps, m, cosv, sinv, kv_dst, vdst0, vdst1)
        q_tr(qk, m, qdst, kdst)

    def finalize_heads(accv, dstv, hp, n0, n, small=False):
        if small:
            tmp, rd = fz_tmp, fz_rd
            P.copy("vector", tmp[0:64, 0:n], accv[64:128, 0, n0:n0 + n])
            P.copy("vector", tmp[64:128, 0:n], accv[0:64, 1, n0:n0 + n])
            P.recip(rd[:, 0:n], tmp[:, 0:n])
            P.tt("vector", dstv[0:64, hp, n0:n0 + n], accv[0:64, 0, n0:n0 + n], rd[0:64, 0:n], ALU.mult)
            P.tt("vector", dstv[64:128, hp, n0:n0 + n], accv[64:128, 1, n0:n0 + n], rd[64:128, 0:n], ALU.mult)
            return
        tmp = tf.get()
        P.copy("vector", tmp[0:64, 0:n], accv[64:128, 0, n0:n0 + n])
        P.copy("vector", tmp[64:128, 0:n], accv[0:64, 1, n0:n0 + n])
        rd = tf.get()
        P.recip(rd[:, 0:n], tmp[:, 0:n])
        P.tt("vector", dstv[0:64, hp, n0:n0 + n], accv[0:64, 0, n0:n0 + n], rd[0:64, 0:n], ALU.mult)
        P.tt("vector", dstv[64:128, hp, n0:n0 + n], accv[64:128, 1, n0:n0 + n], rd[64:128, 0:n], ALU.mult)

    n_layers = DEV.get("layers", L)
    n_seq = DEV.get("seqs", NSEQ)

    def record_all():
        P.dma("sync", ident_f[:], c_ident[:, :])
        P.dma("gpsimd", ident_b[:], c_ident[:, :])
        P.dma("gpsimd", tril_b[:], c_tril[:, :])
        P.dma("gpsimd", mask_b[:], c_mask[:, :])
        P.dma("gpsimd", masks_b[:], c_masks[:, :])
        P.dma("sync", cs_s[:], c_cs_s[:, :])
        P.memset("gpsimd", ones_b[:], 1.0)
        P.memset("gpsimd", mhalf[:], -0.5)
        P.memset("gpsimd", eps_t[:], EPS)
        P.memset("gpsimd", zeros2[:], 0.0)
        P.memset("gpsimd", vnew[:], 1.0)
        for b in range(2):
            P.memset("gpsimd", Vc[b][:], 1.0)
        for s in range(n_seq):
            with_sample = (s == 0) and DEV.get("sample", True)
            record_seq(s, with_sample)

    xpend = {}

    def issue_x_block(s, blk):
        xi = xin.get()
        P.dma("sync", xi, xp[s, blk * 128:(blk + 1) * 128, :])
        xpend[(s, blk)] = xi

    def load_x_block(s, blk):
        if (s, blk) not in xpend:
            issue_x_block(s, blk)
        xi = xpend.pop((s, blk))
        for half in range(2):
            ps = PS()
            for k4 in range(4):
                kc = half * 4 + k4
                P.tr(ps[:, k4 * 128:(k4 + 1) * 128], xi[:, kc * 128:(kc + 1) * 128], ident_f[:])
            P.copy("scalar" if half else "vector", xT[:, half * 4:(half + 1) * 4, blk * 128:(blk + 1) * 128],
                   ps[:].rearrange("p (k t) -> p k t", k=4))

    def store_y_block(s, blk):
        xo = xin.get()
        for half in range(2):
            ps = PS()
            for k4 in range(4):
                kc = half * 4 + k4
                P.tr(ps[:, k4 * 128:(k4 + 1) * 128], xT[:, kc, blk * 128:(blk + 1) * 128], ident_f[:])
            P.copy("scalar" if half else "vector", xo[:, half * 512:(half + 1) * 512], ps[:])
        P.dma("sync", y[s, blk * 128:(blk + 1) * 128, :], xo)

    def record_seq(s, with_sample):
        if s == 0:
            for blk in range(16):
                for ahead in range(blk, min(blk + 4, 16)):
                    if (s, ahead) not in xpend:
                        issue_x_block(s, ahead)
                load_x_block(s, blk)
        if with_sample:
            xi = xin.get()
            P.dma("sync", xi[0:NS, :], xs[:, :])
            for half in range(2):
                ps = PS()
                for k4 in range(4):
                    kc = half * 4 + k4
                    P.tr(ps[:, k4 * NS:(k4 + 1) * NS], xi[0:NS, kc * 128:(kc + 1) * 128], ident_f[0:NS, 0:NS])
                P.copy("vector", xTs[:, half * 4:(half + 1) * 4, :],
                       ps[:, 0:4 * NS].rearrange("p (k t) -> p k t", k=4))
        for l in range(n_layers):
            record_layer(s, l, with_sample)
        for blk in range(16):
            if s + 1 < n_seq:
                for ahead in range(blk, min(blk + 3, 16)):
                    if (s + 1, ahead) not in xpend:
                        issue_x_block(s + 1, ahead)
            store_y_block(s, blk)
            if s + 1 < n_seq:
                load_x_block(s + 1, blk)
        if with_sample:
            xo = xin.get()
            for half in range(2):
                ps = PS()
                for k4 in range(4):
                    kc = half * 4 + k4
                    P.tr(ps[0:NS, k4 * 128:(k4 + 1) * 128], xTs[:, kc, :], ident_f[:])
                P.copy("vector", xo[0:NS, half * 512:(half + 1) * 512], ps[0:NS, :])
            P.dma("sync", ys[:, :], xo[0:NS, :])

    def record_layer(s, l, with_sample):
        load_layer_params(l)
        tiles = [Tile(False, i) for i in range(4)] + ([Tile(True, 0)] if with_sample else [])

        for tile in tiles:
            stage_norm(buf(tile, xT, xTs), buf(tile, xnT, xnTs), gm_sb, tile)

        if DEV.get("gmlp", True):
            for uh in range(2):
                sl = load_slab(lambda t, uh=uh: [(t[:].rearrange("p (k c) -> p k c", k=8),
                                                  w_in[l, :, uh * 256:(uh + 1) * 256].rearrange("(k p) c -> p k c", p=128))])
                slv = sl[:].rearrange("p (k c) -> p k c", k=8)
                for o2 in range(2):
                    oc = uh * 2 + o2
                    for tile in tiles:
                        n = tile.n
                        ps = PS()
                        for kc in range(8):
                            P.mm(ps[:, 0:n], lhsT=slv[:, kc, o2 * 128:(o2 + 1) * 128],
                                 rhs=tile.sl(buf(tile, xnT, xnTs), kc), start=(kc == 0), stop=(kc == 7))
                        P.act(tile.sl(buf(tile, aT, aTs), oc), ps[:, 0:n], AF.Gelu_apprx_tanh)
            slv2 = []
            for vh in range(2):
                sl = load_slab(lambda t, vh=vh: [(t[:].rearrange("p (k c) -> p k c", k=4),
                                                  w_in[l, vh * 512:(vh + 1) * 512, 512:1024].rearrange("(k p) c -> p k c", p=128))])
                slv2.append(sl[:].rearrange("p (k c) -> p k c", k=4))

            def v_mm(xn_cols, m):
                ps = PS()
                for kc in range(8):
                    P.mm(ps[0:m, :], lhsT=xn_cols(kc), rhs=slv2[kc // 4][:, kc % 4, :],
                         start=(kc == 0), stop=(kc == 7))
                return ps

            def v_chain(ps, m):
                gvf = tf.get()
                P.act(gvf[0:m, :], ps[0:m, :], AF.Gelu_apprx_tanh)
                junk = tf.get()
                ss = tsm.get()
                P.act(junk[0:m, :], gvf[0:m, :], AF.Square, accum_out=ss[0:m, 0:1])
                rs = rs_small(ss[0:m, 0:1], m, 1, 1.0 / 512)
                return gvf, rs

            def v_branch(xn_cols, m):
                return v_chain(v_mm(xn_cols, m), m)

            def v_stage2(blk, ps):
                gvf, rs = v_chain(ps, 128)
                va = tb.get()
                P.stt(va[:], gvf[:], rs[:, 0:1], gv_sb[:], ALU.mult, ALU.mult)
                return va

            def v_stage3(blk, va):
                ps = PS()
                for g in range(8):
                    j, hf = g // 2, g % 2
                    P.mm(ps[hf * 64:(hf + 1) * 64, j * 128:(j + 1) * 128],
                         lhsT=va[:, g * 64:(g + 1) * 64], rhs=ws_b[:, g, :], start=True, stop=True)
                zb = tf.get()
                zv = zb[:].rearrange("p (j t) -> p j t", j=4)
                P.tt("vector", zv, ps[:].rearrange("p (j t) -> p j t", j=4), bT_sb, ALU.add)
                av = aT[:, :, blk * 128:(blk + 1) * 128]
                P.tt("gpsimd", av, zv, av, ALU.mult)

            vst = {}
            for step in range(16 + 2):
                if step < 16:
                    vst[step] = v_mm(lambda kc, step=step: xnT[:, kc, step * 128:(step + 1) * 128], 128)
                if 0 <= step - 1 < 16:
                    vst[step - 1] = v_stage2(step - 1, vst[step - 1])
                if 0 <= step - 2 < 16:
                    v_stage3(step - 2, vst[step - 2])
            if with_sample:
                gvf, rs = v_branch(lambda kc: xnTs[:, kc, :], NS)
                vaf = tf.get()
                P.stt(vaf[0:NS, :], gvf[0:NS, :], rs[0:NS, 0:1], gv_sb[0:NS, :], ALU.mult, ALU.mult)
                P.dma("sync", vch[l], vaf[0:NS, :])
                ps = PS()
                for j in range(4):
                    P.tr(ps[:, j * NS:(j + 1) * NS], vaf[0:NS, j * 128:(j + 1) * 128], ident_f[0:NS, 0:NS])
                zb = tf.get()
                zv = zb[:, 0:4 * NS].rearrange("p (j t) -> p j t", j=4)
                P.tt("vector", zv, ps[:, 0:4 * NS].rearrange("p (j t) -> p j t", j=4),
                     w00_sb[:].unsqueeze(2).to_broadcast([128, 4, NS]), ALU.mult)
                zb2 = tf.get()
                zv2 = zb2[:, 0:4 * NS].rearrange("p (j t) -> p j t", j=4)
                P.tt("vector", zv2, zv, b00_sb[:].unsqueeze(2).to_broadcast([128, 4, NS]), ALU.add)
                P.tt("vector", aTs[:], zv2, aTs[:], ALU.mult)

        if DEV.get("attn", True):
            glist = [(hp, i) for hp in range(2) for i in range(3)][:DEV.get("ngroups", 6)]
            tabs = [(cos_sb[:].rearrange("p a b -> p (a b)"), sin_sb[:].rearrange("p a b -> p (a b)")),
                    (gpar[:, 0:512], gpar[:, 512:1024])]
            GS = {}

            def qkv_slabs(g):
                slA = load_slab(lambda t, g=g: [(t[:, 0:1920].rearrange("p (k c) -> p k c", k=5),
                                                 wqkv[l, 0:640, g * 384:(g + 1) * 384].rearrange("(k p) c -> p k c", p=128))])
                slB = load_slab(lambda t, g=g: [(t[:, 0:1152].rearrange("p (k c) -> p k c", k=3),
                                                 wqkv[l, 640:1024, g * 384:(g + 1) * 384].rearrange("(k p) c -> p k c", p=128))])
                return (slA[:, 0:1920].rearrange("p (k c) -> p k c", k=5),
                        slB[:, 0:1152].rearrange("p (k c) -> p k c", k=3))

            def open_group(gi_):
                hp, i = glist[gi_]
                win, dil = PAIRS[i]
                slA, slB = qkv_slabs(i * 2 + hp)
                cf, sf = tabs[gi_ % 2]
                P.dma("sync", cf, c_cos[i])
                P.dma("sync", sf, c_sin[i])
                if gi_ == 0:
                    P.memset("gpsimd", Vb[:, :, 0, 64:128], 1.0)
                    P.memset("gpsimd", Vb[:, :, 1, 0:64], 1.0)
                GS[gi_] = dict(hp=hp, i=i, win=win, dil=dil, slA=slA, slB=slB,
                               cos=cf.rearrange("p (a b) -> p a b", a=16), sin=sf.rearrange("p (a b) -> p a b", a=16))

            def blk_info(G, blk):
                dil, win, hp, i = G["dil"], G["win"], G["hp"], G["i"]
                n_, r_ = blk // dil, blk % dil
                base = dil * 128 * n_ + r_
                kv_dst = None
                t0 = T - win
                if base >= t0:
                    row0 = base - t0
                    dd = kvp[i][l, s]
                    kd = bass.AP(dd.tensor, dd.offset + row0 * 512 + hp * 128, [[dil * 512, 128], [1, 128]])
                    vd = bass.AP(dd.tensor, dd.offset + row0 * 512 + 256 + hp * 128, [[dil * 512, 128], [1, 128]])
                    kv_dst = (kd, vd)
                tok = (lambda kc, base=base, dil=dil: xnT[:, kc, base:base + 127 * dil + 1:dil])
                return tok, kv_dst, base

            def st_mm(G, blk):
                tok, _, _ = blk_info(G, blk)
                return q_mm(G["slA"], G["slB"], tok, 128, pipelined=True)

            def st_A(G, blk, ps):
                return q_chA(ps, 128)

            def st_B(G, blk, stA):
                return q_chB(stA, 128, G["cos"][:, blk, :], G["sin"][:, blk, :],
                             Vb[:, blk, 0, 0:64], Vb[:, blk, 1, 64:128])

            def st_C(G, blk, stB):
                _, kv_dst, _ = blk_info(G, blk)
                return q_chC(stB, 128, kv_dst)

            def st_tr(G, blk, qk):
                q_tr(qk, 128, qTb[:, blk * 128:(blk + 1) * 128], kTb[:, blk * 128:(blk + 1) * 128])

            def key_blocks(G, blk):
                dil = G["dil"]
                return ([blk - dil] if blk // dil >= 1 else []) + [blk]

            def st_qk(G, blk):
                kbs = key_blocks(G, blk)
                nk = len(kbs)
                pT = tbp.get()
                for hh in range(2):
                    pss = psa.get()
                    if nk == 2:
                        P.mm(pss[:, 0:256], lhsT=ident_b[:], rhs=mask_b[:, 0:256], start=True, stop=False)
                    else:
                        P.mm(pss[:, 0:128], lhsT=ident_b[:], rhs=mask_b[:, 128:256], start=True, stop=False)
                    for ki, kb in enumerate(kbs):
                        P.mm(pss[:, ki * 128:(ki + 1) * 128],
                             lhsT=kTb[hh * 64:(hh + 1) * 64, kb * 128:(kb + 1) * 128],
                             rhs=qTb[hh * 64:(hh + 1) * 64, blk * 128:(blk + 1) * 128],
                             start=False, stop=(ki == nk - 1))
                    P.act(pT[:, hh * nk * 128:(hh + 1) * nk * 128], pss[:, 0:nk * 128], AF.Exp, scale=0.125)
                return pT

            def st_pv(G, blk, pT):
                kbs = key_blocks(G, blk)
                nk = len(kbs)
                dil = G["dil"]
                _, _, base = blk_info(G, blk)
                pso = psa.get()
                for hh in range(2):
                    for ki, kb in enumerate(kbs):
                        c0 = (hh * nk + ki) * 128
                        P.mm(pso[:, hh * 128:(hh + 1) * 128], lhsT=Vb[:, kb, hh, :],
                             rhs=pT[:, c0:c0 + 128], start=(ki == 0), stop=(ki == nk - 1))
                dst = acc[:, :, base:base + 127 * dil + 1:dil]
                src = pso[:, 0:256].rearrange("p (h q) -> p h q", h=2)
                if G["i"] == 0:
                    P.copy("scalar", dst, src)
                else:
                    P.tt("vector", dst, src, dst, ALU.add)

            NV = len(glist) * 16
            stv = {}
            stages = [st_mm, st_A, st_B, st_C, st_tr, st_qk, st_pv]
            for step in range(NV + 6):
                for si in (0, 3, 1, 2, 4, 5, 6):
                    fn = stages[si]
                    v = step - si
                    if not (0 <= v < NV):
                        continue
                    gi_, blk = divmod(v, 16)
                    if si == 0 and blk == 0:
                        open_group(gi_)
                    G = GS[gi_]
                    if si == 0:
                        stv[v] = fn(G, blk)
                    elif si in (4, 6):
                        fn(G, blk, stv[v])
                    elif si == 5:
                        stv[v] = fn(G, blk)
                    else:
                        stv[v] = fn(G, blk, stv[v])
                c = step - (16 * 3 + 5)
                if len(glist) == 6 and 0 <= c < 16:
                    finalize_heads(acc, bTo, 0, c * 128, 128, small=True)
            if len(glist) == 6:
                for tl in range(4):
                    finalize_heads(acc, bTo, 1, tl * 512, 512)

            if with_sample:
                for hp in range(2):
                    for i, (win, dil) in enumerate(PAIRS):
                        slA, slB = qkv_slabs(i * 2 + hp)

                        def load_cache(b, i=i, dil=dil, hp=hp):
                            kc_ = kcb[b % 2]
                            vc_ = Vc[b % 2]
                            src = ck[i][l, b]
                            P.dma("gpsimd", kc_[:], bass.AP(src.tensor, src.offset + hp * 128, [[dil * 512, 128], [1, 128]]))
                            P.dma("gpsimd", vc_[:, 0, 0:64],
                                  bass.AP(src.tensor, src.offset + 256 + hp * 128, [[dil * 512, 128], [1, 64]]))
                            P.dma("gpsimd", vc_[:, 1, 64:128],
                                  bass.AP(src.tensor, src.offset + 256 + hp * 128 + 64, [[dil * 512, 128], [1, 64]]))

                        load_cache(0)
                        load_cache(1)
                        qkv_block(slA, slB, lambda kc: xnTs[:, kc, :], NS, cs_s[0:NS, 0:32], cs_s[0:NS, 32:64],
                                  (kvs[i][l, :, hp * 128:(hp + 1) * 128], kvs[i][l, :, 256 + hp * 128:256 + (hp + 1) * 128]),
                                  vnew[0:NS, 0, 0:64], vnew[0:NS, 1, 64:128], qTs[:, :], kTs[:, :])
                        for hh in range(2):
                            pss = PS()
                            P.mm(pss[0:NS, 0:4], lhsT=ident_b[0:NS, 0:NS], rhs=masks_b[0:NS, 32 + hh * 4:36 + hh * 4],
                                 start=True, stop=False)
                            P.mm(pss[0:NS, 0:4], lhsT=kTs[hh * 64:(hh + 1) * 64, :],
                                 rhs=qTs[hh * 64:(hh + 1) * 64, :], start=False, stop=True)
                            P.act(pTs[0:NS, NS, hh * 4:(hh + 1) * 4], pss[0:NS, 0:4], AF.Exp, scale=0.125)
                        first = [i == 0]

                        def acc_add(pso):
                            srcs = pso[:, 0:8].rearrange("p (h q) -> p h q", h=2)
                            if first[0]:
                                P.copy("vector", accs[:], srcs)
                                first[0] = False
                            else:
                                P.tt("vector", accs[:], srcs, accs[:], ALU.add)

                        pso = PS()
                        for hh in range(2):
                            P.mm(pso[:, hh * 4:(hh + 1) * 4], lhsT=vnew[0:NS, hh, :], rhs=pTs[0:NS, NS, hh * 4:(hh + 1) * 4],
                                 start=True, stop=True)
                        acc_add(pso)
                        for b in range(NS):
                            kc_ = kcb[b % 2]
                            vc_ = Vc[b % 2]
                            pt = PSB()
                            P.tr(pt[:, 0:128], kc_[:], ident_b[:])
                            kT_ = kcT[b % 2]
                            P.copy("vector", kT_[:], pt[:, 0:128])
                            for hh in range(2):
                                pss = PS()
                                P.mm(pss[:, 0:4], lhsT=ident_b[:], rhs=masks_b[:, b * 8 + hh * 4:b * 8 + hh * 4 + 4],
                                     start=True, stop=False)
                                P.mm(pss[:, 0:4], lhsT=kT_[hh * 64:(hh + 1) * 64, :],
                                     rhs=qTs[hh * 64:(hh + 1) * 64, :], start=False, stop=True)
                                P.act(pTs[:, b, hh * 4:(hh + 1) * 4], pss[:, 0:4], AF.Exp, scale=0.125)
                            pso = PS()
                            for hh in range(2):
                                P.mm(pso[:, hh * 4:(hh + 1) * 4], lhsT=vc_[:, hh, :], rhs=pTs[:, b, hh * 4:(hh + 1) * 4],
                                     start=True, stop=True)
                            acc_add(pso)
                            if b + 2 < NS:
                                load_cache(b + 2)
                    finalize_heads(accs, bTos, hp, 0, NS)

        if DEV.get("merge", True):
            for jc in range(8):
                sga_ = load_slab(lambda t, jc=jc: [(t[:, 0:1024], wmrg[l, jc][:, 0:1024])])
                sgb_ = load_slab(lambda t, jc=jc: [(t[:, 0:1024], wmrg[l, jc][:, 1024:2048])])
                sab_ = load_slab(lambda t, jc=jc: [(t[:, 0:768], wmrg[l, jc][:, 2048:2816])])
                wga = sga_[:, 0:1024].rearrange("p (k c) -> p k c", k=8)
                wgb = sgb_[:, 0:1024].rearrange("p (k c) -> p k c", k=8)
                wab = sab_[:, 0:768].rearrange("p (k c) -> p k c", k=6)
                for tile in tiles:
                    n = tile.n
                    pga, pgb, pa, pb = PS(), PS(), PS(), PS()
                    xn_ = buf(tile, xnT, xnTs)
                    for kc in range(8):
                        P.mm(pga[:, 0:n], lhsT=wga[:, kc, :], rhs=tile.sl(xn_, kc), start=(kc == 0), stop=(kc == 7))
                    for kc in range(8):
                        P.mm(pgb[:, 0:n], lhsT=wgb[:, kc, :], rhs=tile.sl(xn_, kc), start=(kc == 0), stop=(kc == 7))
                    for kc in range(4):
                        P.mm(pa[:, 0:n], lhsT=wab[:, kc, :], rhs=tile.sl(buf(tile, aT, aTs), kc),
                             start=(kc == 0), stop=(kc == 3))
                    for kc in range(2):
                        P.mm(pb[:, 0:n], lhsT=wab[:, 4 + kc, :], rhs=tile.sl(buf(tile, bTo, bTos), kc),
                             start=(kc == 0), stop=(kc == 1))
                    sga, sgb = tf.get(), tf.get()
                    P.act(sga[:, 0:n], pga[:, 0:n], AF.Sigmoid)
                    P.act(sgb[:, 0:n], pgb[:, 0:n], AF.Sigmoid)
                    P.tt("vector", sga[:, 0:n], sga[:, 0:n], pa[:, 0:n], ALU.mult)
                    P.tt("vector", sgb[:, 0:n], sgb[:, 0:n], pb[:, 0:n], ALU.mult)
                    P.tt("gpsimd", tile.sl(buf(tile, mT, mTs), jc), sga[:, 0:n], sgb[:, 0:n], ALU.add)
            for oc in range(8):
                sl = load_slab(lambda t, oc=oc: [(t[:, 0:1024], wo_r[l, oc])])
                slv = sl[:, 0:1024].rearrange("p (k c) -> p k c", k=8)
                for tile in tiles:
                    n = tile.n
                    ps = PS()
                    for kc in range(8):
                        P.mm(ps[:, 0:n], lhsT=slv[:, kc, :], rhs=tile.sl(buf(tile, mT, mTs), kc),
                             start=(kc == 0), stop=(kc == 7))
                    xd = tile.sl(buf(tile, xT, xTs), oc)
                    P.tt("vector", xd, ps[:, 0:n], xd, ALU.add)

        if DEV.get("ffn", True):
            for tile in tiles:
                stage_norm(buf(tile, xT, xTs), buf(tile, xnT, xnTs), gf_sb, tile)
            if with_sample:
                P.dma("sync", cvs[l, :, 0, :], stc[l, :, 1, :])
                for q4 in range(11):
                    stg = tf.get()
                    P.dma("sync", stg[0:2 * NS, :], stc[l, :, :, q4 * 512:(q4 + 1) * 512].rearrange("b j c -> (b j) c"))
                    ps = PS()
                    for k4 in range(4):
                        P.tr(ps[:, k4 * 8:(k4 + 1) * 8], stg[0:2 * NS, k4 * 128:(k4 + 1) * 128], ident_f[0:2 * NS, 0:2 * NS])
                    for k4 in range(4):
                        P.copy("vector", stTs[:, q4 * 4 + k4, :, :],
                               ps[:, k4 * 8:(k4 + 1) * 8].rearrange("p (b j) -> p j b", j=2))
            for gi in range(2):
                P.copy("scalar", cry[gi][0][:, 0:2], zeros2[:])
            groups = [(0, 8), (8, 8), (16, 6)]
            for (c_lo, gn) in groups:
                def hsl(tile, c):
                    if tile.sample:
                        return hTs[:, c, :]
                    return mT[:, c, tile.i * 512:(tile.i + 1) * 512]

                for c in range(gn):
                    cc = c_lo + c
                    sl = load_slab(lambda t, cc=cc: [(t[:, 0:1024], wup_r[l, cc][:, 0:1024]), (t[:, 1024:2048], wup_r[l, cc][:, 1024:2048])])
                    slv = sl[:].rearrange("p (g k c) -> p g k c", g=2, k=8)
                    for tile in tiles:
                        n = tile.n
                        cres = []
                        for gi in range(2):
                            ch = gi * NCH + cc
                            ps = PS()
                            for kc in range(8):
                                P.mm(ps[:, 0:n], lhsT=slv[:, gi, kc, :], rhs=tile.sl(buf(tile, xnT, xnTs), kc),
                                     start=(kc == 0), stop=(kc == 7))
                            c0 = tf.get()
                            P.act(c0[:, 0:n], ps[:, 0:n], AF.Identity, scale=cw_sb[:, 2, ch:ch + 1],
                                  bias=cb_sb[:, ch:ch + 1])
                            if tile.sample:
                                P.copy("scalar", upls[:, ch, :], ps[:, 0:n])
                                P.stt(c0[:, 0:n], stTs[:, ch, 1, :], cw_sb[:, 1, ch:ch + 1], c0[:, 0:n], ALU.mult, ALU.add)
                                P.stt(c0[:, 0:n], stTs[:, ch, 0, :], cw_sb[:, 0, ch:ch + 1], c0[:, 0:n], ALU.mult, ALU.add)
                            else:
                                cr = cry[gi][tile.i % 2]
                                nxt = cry[gi][(tile.i + 1) % 2]
                                w1c, w0c = cw_sb[:, 1, ch:ch + 1], cw_sb[:, 0, ch:ch + 1]
                                if tile.i == 3:
                                    P.copy("scalar", upl[:, ch, :], ps[:, 510:512])
                                    P.copy("scalar", nxt[:, 0:2], zeros2[:])
                                else:
                                    P.act(nxt[:, 0:2], ps[:, 510:512], AF.Copy, scale=w0c)
                                    P.act(nxt[:, 0:1], ps[:, 511:512], AF.Identity, scale=w1c, bias=nxt[:, 0:1])
                                P.stt(c0[:, 1:512], ps[:, 0:511], w1c, c0[:, 1:512], ALU.mult, ALU.add)
                                P.stt(c0[:, 2:512], ps[:, 0:510], w0c, c0[:, 2:512], ALU.mult, ALU.add)
                                P.tt("vector", c0[:, 0:2], c0[:, 0:2], cr[:, 0:2], ALU.add)
                            cres.append(c0)
                        sg = tb.get()
                        P.act(sg[:, 0:n], cres[0][:, 0:n], AF.Silu)
                        P.tt("gpsimd", hsl(tile, c), sg[:, 0:n], cres[1][:, 0:n], ALU.mult)
                for oc in range(8):
                    def mk(t, oc=oc, c_lo=c_lo, gn=gn):
                        a = w_dn[l]
                        src = bass.AP(a.tensor, a.offset + c_lo * 128 * D + oc * 128, [[D, 128], [128 * D, gn], [1, 128]])
                        return [(t[:, 0:gn * 128].rearrange("p (c k) -> p c k", c=gn), src)]
                    sl = load_slab(mk)
                    slv = sl[:, 0:gn * 128].rearrange("p (c k) -> p c k", c=gn)
                    for tile in tiles:
                        n = tile.n
                        ps = PS()
                        for c in range(gn):
                            P.mm(ps[:, 0:n], lhsT=slv[:, c, :], rhs=hsl(tile, c), start=(c == 0), stop=(c == gn - 1))
                        xd = tile.sl(buf(tile, xT, xTs), oc)
                        P.tt("vector", xd, ps[:, 0:n], xd, ALU.add)
            for q4 in range(11):
                ps = PS()
                for k4 in range(4):
                    P.tr(ps[0:2, k4 * 128:(k4 + 1) * 128], upl[:, q4 * 4 + k4, :], ident_f[:])
                stg = tf.get()
                P.copy("vector", stg[0:2, :], ps[0:2, :])
                P.dma("sync", cvp[l, s, :, q4 * 512:(q4 + 1) * 512], stg[0:2, :])
                if with_sample:
                    ps = PS()
                    for k4 in range(4):
                        P.tr(ps[0:NS, k4 * 128:(k4 + 1) * 128], upls[:, q4 * 4 + k4, :], ident_f[:])
                    stg = tf.get()
                    P.copy("vector", stg[0:NS, :], ps[0:NS, :])
                    P.dma("sync", cvs[l, :, 1, q4 * 512:(q4 + 1) * 512], stg[0:NS, :])

    P.dry = True
    st["dry"] = True
    record_all()
    P.dry = False
    st["dry"] = False
    st["seq"] = 0
    record_all()

    P.emit_all(es)
    es.close()
    return nc


def _host_consts():
    ident = np.eye(128, dtype=np.float32)
    tril = (np.arange(128)[:, None] <= np.arange(128)[None, :]).astype(np.float32)
    k = np.arange(128)[:, None]
    q = np.arange(128)[None, :]
    prev = np.where(k >= q, 0.0, NEG).astype(np.float32)
    cur = np.where(k <= q, 0.0, NEG).astype(np.float32)
    one = np.concatenate([prev, cur], axis=1)
    mask = np.concatenate([one, one], axis=1)
    masks = np.full((128, 40), NEG, np.float32)
    for b in range(NS):
        for hh in range(2):
            masks[:, b * 8 + hh * 4 + b] = 0.0
    for hh in range(2):
        for b in range(NS):
            masks[b, 32 + hh * 4 + b] = 0.0
    half = 32
    inv = (10000.0 ** (-np.arange(half, dtype=np.float32) / half)).astype(np.float32)
    cos = np.zeros((3, 128, 16, 32), np.float32)
    sin = np.zeros((3, 128, 16, 32), np.float32)
    for i, (win, dil) in enumerate(PAIRS):
        for blk in range(16):
            n_, r_ = blk // dil, blk % dil
            pos = (dil * 128 * n_ + r_ + dil * np.arange(128)).astype(np.float32)
            ang = pos[:, None] * inv[None, :]
            c, s_ = np.cos(ang), np.sin(ang)
            cos[i, :, blk, :] = c
            sin[i, :, blk, :] = s_
    ang = np.float32(PAST) * inv
    cs = np.concatenate([np.cos(ang), np.sin(ang)]).astype(np.float32)
    cs_s = np.tile(cs[None, :], (NS, 1))
    return dict(c_ident=ident, c_tril=tril, c_mask=mask, c_masks=masks,
                c_cos=cos.reshape(3, 128, 512), c_sin=sin.reshape(3, 128, 512), c_cs_s=cs_s)


def _host_weights(inp):
    f = lambda a: np.ascontiguousarray(np.asarray(a, dtype=np.float32))
    w_in = f(inp["w_in"])
    o_q, o_k, o_v, o_ga, o_gb = 1024, 1792, 2560, 3328, 4352
    cols = []
    for g in range(6):
        i, hp = g // 2, g % 2
        h0 = (i * 4 + hp * 2) * 64
        for o in (o_q, o_k, o_v):
            cols.append(np.arange(o + h0, o + h0 + 128))
    cols = np.concatenate(cols)
    wqkv = f(w_in[:, :, cols])
    wa, wb = f(inp["w_a_proj"]), f(inp["w_b_proj"])
    wmrg = np.zeros((L, 8, 128, 22, 128), np.float32)
    for jc in range(8):
        cs_ = slice(jc * 128, (jc + 1) * 128)
        ga = w_in[:, :, o_ga:o_gb][:, :, cs_].reshape(L, 8, 128, 128)
        gb = w_in[:, :, o_gb:][:, :, cs_].reshape(L, 8, 128, 128)
        wmrg[:, jc, :, 0:8] = ga.transpose(0, 2, 1, 3)
        wmrg[:, jc, :, 8:16] = gb.transpose(0, 2, 1, 3)
        wmrg[:, jc, :, 16:20] = wa[:, :, cs_].reshape(L, 4, 128, 128).transpose(0, 2, 1, 3)
        wmrg[:, jc, :, 20:22] = wb[:, :, cs_].reshape(L, 2, 128, 128).transpose(0, 2, 1, 3)
    wmrg = wmrg.reshape(L, 8, 128, 2816)
    wo = f(inp["w_o"]).reshape(L, 8, 128, 8, 128)
    wo_r = f(wo.transpose(0, 3, 2, 1, 4)).reshape(L, 8, 128, 1024)
    wup = f(inp["w_up"]).reshape(L, 8, 128, 2, NCH, 128)
    wup_r = f(wup.transpose(0, 4, 2, 3, 1, 5)).reshape(L, NCH, 128, 2048)
    gm = f(f(inp["g_mix"]).reshape(L, 8, 128).transpose(0, 2, 1))
    gf = f(f(inp["g_ffn"]).reshape(L, 8, 128).transpose(0, 2, 1))
    gqk = np.concatenate([f(inp["g_q"])] * 2 + [f(inp["g_k"])] * 2, axis=1)
    ws_ = f(inp["w_s"])
    wsT = f(ws_.transpose(0, 3, 1, 2)).reshape(L, 128, 1024)
    bs = f(inp["b_s"])
    bT = f(np.repeat(bs.reshape(L, 4, 2, 1, 128), 64, axis=3).reshape(L, 4, 128, 128).transpose(0, 2, 1, 3))
    bT = bT.reshape(L, 128, 512)
    cwv = f(inp["conv_w"]).reshape(L, 3, 44, 128)
    cw = f(cwv.transpose(0, 3, 1, 2)).reshape(L, 128, 132)
    cb = f(f(inp["conv_b"]).reshape(L, 44, 128).transpose(0, 2, 1))
    w00 = f(np.repeat(ws_[:, :, 0, 0].reshape(L, 4, 2, 1), 64, axis=3).reshape(L, 4, 128).transpose(0, 2, 1))
    b00 = f(np.repeat(bs[:, :, 0].reshape(L, 4, 2, 1), 64, axis=3).reshape(L, 4, 128).transpose(0, 2, 1))
    return dict(w_in=w_in, wqkv=wqkv, wmrg=wmrg, wo_r=wo_r, wup_r=wup_r, w_dn=f(inp["w_down"]),
                gm=gm, gf=gf, gv=f(inp["g_v"]), gqk=f(gqk), wsT=wsT, bT=bT, cw=cw, cb=cb, w00=w00, b00=b00)


_NC_CACHE = {}


def kernel(**inp):
    shared = _host_weights(inp)
    shared.update(_host_consts())
    xp = np.asarray(inp["x_prompt"], np.float32)
    xs = np.asarray(inp["x_sample"], np.float32)
    cks = [np.asarray(inp[k], np.float32) for k in ("cache_kv_w128", "cache_kv_w512", "cache_kv_w2048")]
    stc = np.asarray(inp["state_conv"], np.float32)
    in_maps = []
    for c in range(NCORES):
        m = dict(shared)
        m["xp"] = np.ascontiguousarray(xp[2 * c:2 * c + 2])
        m["xs"] = np.ascontiguousarray(xs[4 * c:4 * c + 4, 0])
        for i in range(3):
            a = cks[i][:, 4 * c:4 * c + 4]
            m["ck%d" % i] = np.ascontiguousarray(a.reshape(L, NS, a.shape[2], 512))
        m["stc"] = np.ascontiguousarray(stc[:, 4 * c:4 * c + 4])
        in_maps.append(m)
    key = tuple(sorted(DEV.items()))
    if key not in _NC_CACHE:
        _NC_CACHE[key] = build_program()
    nc = _NC_CACHE[key]
    res = run_bass_kernel_spmd(nc, in_maps, core_ids=list(range(NCORES)))
    R = res.results
    cat = lambda name, ax: np.concatenate([np.asarray(r[name]) for r in R], axis=ax)
    y = cat("y", 0)
    ys = cat("ys", 0).reshape(32, 1, D)
    outs = [y, ys]
    for i in range(3):
        a = cat("kvp%d" % i, 1)
        outs.append(a.reshape(L, 16, PAIRS[i][0], 2, 4, 64))
    outs.append(cat("cvp", 1))
    for i in range(3):
        outs.append(cat("kvs%d" % i, 1).reshape(L, 32, 1, 2, 4, 64))
    outs.append(cat("cvs", 1))
    outs.append(cat("vch", 1).reshape(L, 32, 1, 512))
    return tuple(np.ascontiguousarray(o.astype(np.float32)) for o in outs)
```

```python
import math
from collections import defaultdict
from contextlib import ExitStack

import numpy as np
import concourse.bass as bass
import concourse.mybir as mybir
from concourse.bass_utils import run_bass_kernel_spmd

F32 = mybir.dt.float32
BF16 = mybir.dt.bfloat16
AF = mybir.ActivationFunctionType
ALU = mybir.AluOpType
AX = mybir.AxisListType

D = 1024
T = 2048
L = 2
NSEQ = 2
NS = 4
DFF = 2816
NCH = 22
EPS = 1e-6
NEG = -30000.0
PAIRS = ((128, 1), (512, 4), (2048, 16))
PAST = 16384
NCORES = 8

DEV = {}


def _esz(dt):
    return mybir.dt.size(dt)


def _region(ap):
    t = ap.tensor
    if type(t).__name__.startswith("DRam"):
        return None
    shape = t.shape
    pstep = 1
    for s in shape[1:]:
        pstep *= s
    esz = _esz(ap.dtype)
    off = ap.offset
    p0 = off // pstep
    f0 = off % pstep
    dims = ap.ap
    npart = dims[0][1] if dims[0][0] != 0 else 1
    ext = 1
    for s, c in dims[1:]:
        ext += (c - 1) * abs(s)
    return (type(t).__name__[0] + t.name, p0, p0 + npart, f0 * esz, (f0 + ext) * esz)


class Op:
    __slots__ = ("idx", "engine", "emit", "deps", "dma", "needed", "count", "sem", "val", "prev")

    def __init__(self, idx, engine, emit, dma):
        self.idx = idx
        self.engine = engine
        self.emit = emit
        self.deps = set()
        self.dma = dma
        self.needed = False
        self.count = 0
        self.sem = None
        self.val = 0
        self.prev = None


class Prog:
    ENGS = ("tensor", "scalar", "vector", "gpsimd", "sync")
    NPOOL = 12

    def __init__(self, nc):
        self.nc = nc
        self.ops = []
        self.recs = defaultdict(list)
        self.dry = False

    def add(self, engine, emit, reads=(), writes=(), dma=False):
        if self.dry:
            return None
        op = Op(len(self.ops), engine, emit, dma)
        self.ops.append(op)
        for ap in reads:
            self._access(op, ap, False)
        for ap in writes:
            self._access(op, ap, True)
        return op

    def _access(self, op, ap, is_write):
        reg = _region(ap)
        if reg is None:
            return
        name, p0, p1, lo, hi = reg
        if name[0] == "P":
            p0, p1, lo, hi, is_write = 0, 128, 0, 2048, True
        lst = self.recs[name]
        out = []
        for r in lst:
            if r[1] <= p0 or p1 <= r[0] or r[3] <= lo or hi <= r[2]:
                out.append(r)
                continue
            dop = self.ops[r[4]]
            if dop.idx != op.idx:
                if is_write:
                    same = (dop.engine == op.engine) and not dop.dma and not op.dma and op.engine != "gpsimd"
                    if not same:
                        op.deps.add(dop.idx)
                elif r[5]:
                    op.deps.add(dop.idx)
            covered = r[0] >= p0 and r[1] <= p1 and r[2] >= lo and r[3] <= hi
            if is_write and covered:
                continue
            if (not is_write) and (not r[5]) and covered and dop.engine == op.engine and not dop.dma and not op.dma:
                continue
            out.append(r)
        out.append((p0, p1, lo, hi, op.idx, is_write))
        self.recs[name] = out

    def mm(self, out, lhsT, rhs, start=True, stop=True):
        return self.add("tensor", lambda e: e.matmul(out, lhsT=lhsT, rhs=rhs, start=start, stop=stop),
                        reads=(lhsT, rhs), writes=(out,))

    def tr(self, out, in_, ident):
        return self.add("tensor", lambda e: e.transpose(out=out, in_=in_, identity=ident),
                        reads=(in_, ident), writes=(out,))

    def act(self, out, in_, func, scale=None, bias=None, accum_out=None):
        kw = {}
        reads = [in_]
        writes = [out]
        if scale is not None:
            kw["scale"] = scale
            if not isinstance(scale, (int, float)):
                reads.append(scale)
        if bias is not None:
            kw["bias"] = bias
            if not isinstance(bias, (int, float)):
                reads.append(bias)
        if accum_out is not None:
            kw["accum_out"] = accum_out
            writes.append(accum_out)
        return self.add("scalar", lambda e: e.activation(out=out, in_=in_, func=func, **kw), reads, writes)

    def tt(self, eng, out, in0, in1, op):
        return self.add(eng, lambda e: e.tensor_tensor(out=out, in0=in0, in1=in1, op=op), (in0, in1), (out,))

    def ts(self, eng, out, in0, s1, s2, op0, op1=None):
        reads = [in0]
        for s in (s1, s2):
            if s is not None and not isinstance(s, (int, float)):
                reads.append(s)
        if op1 is None:
            return self.add(eng, lambda e: e.tensor_scalar(out=out, in0=in0, scalar1=s1, scalar2=None, op0=op0),
                            reads, (out,))
        return self.add(eng, lambda e: e.tensor_scalar(out=out, in0=in0, scalar1=s1, scalar2=s2, op0=op0, op1=op1),
                        reads, (out,))

    def stt(self, out, in0, scalar, in1, op0, op1):
        reads = [in0, in1]
        if not isinstance(scalar, (int, float)):
            reads.append(scalar)
        return self.add("vector", lambda e: e.scalar_tensor_tensor(out=out, in0=in0, scalar=scalar, in1=in1,
                                                                    op0=op0, op1=op1), reads, (out,))

    def copy(self, eng, out, in_):
        if eng == "scalar":
            return self.act(out, in_, AF.Copy)
        return self.add(eng, lambda e: e.tensor_copy(out=out, in_=in_), (in_,), (out,))

    def reduce(self, out, in_, op=ALU.add):
        return self.add("vector", lambda e: e.tensor_reduce(out=out, in_=in_, axis=AX.X, op=op), (in_,), (out,))

    def recip(self, out, in_):
        return self.add("vector", lambda e: e.reciprocal(out=out, in_=in_), (in_,), (out,))

    def memset(self, eng, out, val):
        return self.add(eng, lambda e: e.memset(out, val), (), (out,))

    def dma(self, q, out, in_):
        return self.add(q, lambda e: e.dma_start(out=out, in_=in_), (in_,), (out,), dma=True)

    def emit_all(self, es):
        nc = self.nc
        ops = self.ops
        sems = {e: es.enter_context(nc.semaphore("s_" + e)) for e in self.ENGS}
        pools = {q: [es.enter_context(nc.semaphore("d_%s_%d" % (q, i))) for i in range(self.NPOOL)]
                 for q in ("sync", "gpsimd")}
        for op in ops:
            for d in op.deps:
                ops[d].needed = True
        cnt = defaultdict(int)
        didx = defaultdict(int)
        dtot = {q: [0] * self.NPOOL for q in pools}
        dlast = {q: [None] * self.NPOOL for q in pools}
        for op in ops:
            if op.dma:
                q = op.engine
                k = didx[q] % self.NPOOL
                didx[q] += 1
                op.sem = pools[q][k]
                dtot[q][k] += 16
                op.val = dtot[q][k]
                op.prev = dlast[q][k]
                dlast[q][k] = op
            elif op.needed:
                cnt[op.engine] += 1
                op.count = cnt[op.engine]
        per = {e: [op for op in ops if op.engine == e] for e in self.ENGS}
        block = es.enter_context(nc.Block())

        def make_body(e):
            my = per[e]

            def body(eng):
                wm = defaultdict(int)
                dw = {}

                def wait_dma(dop):
                    key = id(dop.sem)
                    if dw.get(key, 0) < dop.val:
                        eng.wait_ge(dop.sem, dop.val)
                        dw[key] = dop.val

                for op in my:
                    for d in sorted(op.deps):
                        dop = ops[d]
                        if dop.dma:
                            wait_dma(dop)
                        elif wm[dop.engine] < dop.count:
                            eng.wait_ge(sems[dop.engine], dop.count)
                            wm[dop.engine] = dop.count
                    if op.dma and op.prev is not None:
                        wait_dma(op.prev)
                    ins = op.emit(eng)
                    if op.dma:
                        ins.then_inc(op.sem, 16)
                    elif op.needed:
                        ins.then_inc(sems[e], 1)
                if e == "sync":
                    for q in pools:
                        for k in range(self.NPOOL):
                            if dtot[q][k] > 0:
                                eng.wait_ge(pools[q][k], dtot[q][k])
            return body

        for e in self.ENGS:
            getattr(block, e)(make_body(e))


def build_program():
    nc = bass.Bass("TRN2", target_bir_lowering=False)
    P = Prog(nc)
    es = ExitStack()

    def din(name, shape, dt=F32):
        return nc.dram_tensor(name, list(shape), dt, kind="ExternalInput").ap()

    def dout(name, shape, dt=F32):
        return nc.dram_tensor(name, list(shape), dt, kind="ExternalOutput").ap()

    def sb(name, shape, dt):
        return es.enter_context(nc.sbuf_tensor(name, list(shape), dt))

    def psum(name, shape, dt):
        return es.enter_context(nc.psum_tensor(name, list(shape), dt))

    xp = din("xp", [NSEQ, T, D])
    xs = din("xs", [NS, D])
    ck = [din("ck%d" % i, [L, NS, PAIRS[i][0], 512]) for i in range(3)]
    stc = din("stc", [L, NS, 2, 2 * DFF])
    w_in = din("w_in", [L, D, 5376])
    wqkv = din("wqkv", [L, D, 6 * 384])
    wmrg = din("wmrg", [L, 8, 128, 2816])
    wo_r = din("wo_r", [L, 8, 128, 1024])
    wup_r = din("wup_r", [L, NCH, 128, 2048])
    w_dn = din("w_dn", [L, DFF, D])
    gm = din("gm", [L, 128, 8])
    gf = din("gf", [L, 128, 8])
    gv = din("gv", [L, 512])
    gqk = din("gqk", [L, 256])
    wsT = din("wsT", [L, 128, 1024])
    bT = din("bT", [L, 128, 512])
    cw = din("cw", [L, 128, 3 * 44])
    cb = din("cb", [L, 128, 44])
    w00 = din("w00", [L, 128, 4])
    b00 = din("b00", [L, 128, 4])
    c_ident = din("c_ident", [128, 128])
    c_tril = din("c_tril", [128, 128])
    c_mask = din("c_mask", [128, 512])
    c_masks = din("c_masks", [128, 4 * 8 + 8])
    c_cos = din("c_cos", [3, 128, 16 * 32])
    c_sin = din("c_sin", [3, 128, 16 * 32])
    c_cs_s = din("c_cs_s", [NS, 64])

    y = dout("y", [NSEQ, T, D])
    ys = dout("ys", [NS, D])
    kvp = [dout("kvp%d" % i, [L, NSEQ, PAIRS[i][0], 512]) for i in range(3)]
    cvp = dout("cvp", [L, NSEQ, 2, 2 * DFF])
    kvs = [dout("kvs%d" % i, [L, NS, 512]) for i in range(3)]
    cvs = dout("cvs", [L, NS, 2, 2 * DFF])
    vch = dout("vch", [L, NS, 512])

    xT = sb("xT", [128, 8, T], F32)
    xnT = sb("xnT", [128, 8, T], BF16)
    aT = sb("aT", [128, 4, T], BF16)
    bTo = sb("bTo", [128, 2, T], BF16)
    ws = sb("ws", [128, 8 * T], BF16)
    ws32 = ws[:].bitcast(F32)
    acc = ws32[:, 0:2 * T].rearrange("p (h t) -> p h t", h=2)
    qTb = ws[:, 4 * T:5 * T]
    kTb = ws[:, 5 * T:6 * T]
    Vb = ws[:, 6 * T:8 * T].rearrange("p (b h c) -> p b h c", b=16, h=2)
    mT = ws[:].rearrange("p (k t) -> p k t", k=8)
    aT32 = aT[:].rearrange("p k t -> p (k t)").bitcast(F32)
    upb = [[aT32[:, (g * 2 + i) * 514:(g * 2 + i + 1) * 514] for i in range(2)] for g in range(2)]

    xTs = sb("xTs", [128, 8, NS], F32)
    xnTs = sb("xnTs", [128, 8, NS], BF16)
    aTs = sb("aTs", [128, 4, NS], BF16)
    bTos = sb("bTos", [128, 2, NS], BF16)
    mTs = sb("mTs", [128, 8, NS], BF16)
    hTs = sb("hTs", [128, 8, NS], BF16)
    accs = sb("accs", [128, 2, NS], F32)
    qTs = sb("qTs", [128, NS], BF16)
    kTs = sb("kTs", [128, NS], BF16)
    vnew = sb("vnew", [NS, 2, 128], BF16)
    kcb = [sb("kcb%d" % i, [128, 128], BF16) for i in range(2)]
    kcT = [sb("kcT%d" % i, [128, 128], BF16) for i in range(2)]
    Vc = [sb("Vc%d" % i, [128, 2, 128], BF16) for i in range(2)]
    pTs = sb("pTs", [128, NS + 1, 8], BF16)
    upls = sb("upls", [128, 44, NS], F32)
    stTs = sb("stTs", [128, 44, 2, NS], F32)
    cs_s = sb("cs_s", [NS, 64], F32)

    NSLAB = 5
    LA = 2
    slab = [sb("slab%d" % i, [128, 2048], BF16) for i in range(NSLAB)]
    gm_sb = sb("gm_sb", [128, 8], F32)
    gf_sb = sb("gf_sb", [128, 8], F32)
    gv_sb = sb("gv_sb", [128, 512], F32)
    gqk_sb = sb("gqk_sb", [128, 256], F32)
    gpar = sb("gpar", [128, 1024], F32)
    ws_b = gpar[:, 0:512].bitcast(BF16).rearrange("p (g t) -> p g t", g=8)
    bT_sb = gpar[:, 512:1024].rearrange("p (a b) -> p a b", a=4)
    cw_sb = sb("cw_sb", [128, 3, 44], F32)
    cb_sb = sb("cb_sb", [128, 44], F32)
    w00_sb = sb("w00_sb", [128, 4], F32)
    b00_sb = sb("b00_sb", [128, 4], F32)
    ident_f = sb("ident_f", [128, 128], F32)
    ident_b = sb("ident_b", [128, 128], BF16)
    ones_b = sb("ones_b", [128, 128], BF16)
    tril_b = sb("tril_b", [128, 128], BF16)
    mask_b = sb("mask_b", [128, 512], BF16)
    masks_b = sb("masks_b", [128, 40], BF16)
    cos_sb = sb("cos_sb", [128, 16, 32], F32)
    sin_sb = sb("sin_sb", [128, 16, 32], F32)
    mhalf = sb("mhalf", [128, 1], F32)
    eps_t = sb("eps_t", [128, 1], F32)
    zeros2 = sb("zeros2", [128, 2], F32)
    upl = sb("upl", [128, 44, 2], F32)
    fz_tmp = sb("fz_tmp", [128, 128], F32)
    fz_rd = sb("fz_rd", [128, 128], F32)
    cry = [[sb("cry%d%d" % (g_, i_), [128, 2], F32) for i_ in range(2)] for g_ in range(2)]

    class Rot:
        def __init__(self, tiles):
            self.t = tiles
            self.i = 0

        def get(self):
            t = self.t[self.i % len(self.t)]
            self.i += 1
            return t

    tf = Rot([sb("tf%d" % i, [128, 512], F32) for i in range(5)])
    tb = Rot([sb("tb%d" % i, [128, 512], BF16) for i in range(4)])
    tsm = Rot([sb("tsm%d" % i, [128, 8], F32) for i in range(12)])
    xin = Rot([ws32[:, i * 1024:(i + 1) * 1024] for i in range(8)])
    tfh = Rot([t[:, h * 256:(h + 1) * 256] for t in tf.t for h in range(2)])
    tbh = Rot([t[:, h * 256:(h + 1) * 256] for t in tb.t[2:4] for h in range(2)])
    tbp = Rot(tb.t[0:2])

    psf = Rot([psum("psf%d" % i, [128, 512], F32) for i in range(6)])
    psb = Rot([psum("psb%d" % i, [128, 1024], BF16) for i in range(2)])
    PS = psf.get
    PSB = psb.get
    psq = Rot(psf.t[0:3])
    psa = Rot(psf.t[3:6])

    st = {"dry": True, "seq": 0, "issued": 0, "plan": []}

    def load_slab(make):
        j = st["seq"]
        st["seq"] += 1
        sl = slab[j % NSLAB]
        if st["dry"]:
            st["plan"].append(make(sl))
        else:
            while st["issued"] < len(st["plan"]) and st["issued"] <= j + LA:
                for dst, src in st["plan"][st["issued"]]:
                    P.dma("gpsimd", dst, src)
                st["issued"] += 1
        return sl

    def bc_row(dram2d, row, n):
        a = dram2d[row:row + 1, :]
        return bass.AP(a.tensor, a.offset, [[0, 128], [1, n]])

    def load_layer_params(l):
        P.dma("sync", gm_sb[:], gm[l])
        P.dma("sync", gf_sb[:], gf[l])
        P.dma("sync", gv_sb[:], bc_row(gv, l, 512))
        P.dma("sync", gqk_sb[:], bc_row(gqk, l, 256))
        P.dma("gpsimd", gpar[:, 0:512].bitcast(BF16), wsT[l])
        P.dma("sync", gpar[:, 512:1024], bT[l])
        P.dma("sync", cw_sb[:].rearrange("p a b -> p (a b)"), cw[l])
        P.dma("sync", cb_sb[:], cb[l])
        P.dma("sync", w00_sb[:], w00[l])
        P.dma("sync", b00_sb[:], b00[l])
        P.tt("vector", ws_b, ws_b, tril_b[:].unsqueeze(1).to_broadcast([128, 8, 128]), ALU.mult)

    class Tile:
        def __init__(self, sample, i):
            self.sample = sample
            self.i = i
            self.n = NS if sample else 512

        def sl(self, buf3, k):
            if self.sample:
                return buf3[:, k, :]
            return buf3[:, k, self.i * 512:(self.i + 1) * 512]

    def buf(tile, prompt_buf, sample_buf):
        return sample_buf if tile.sample else prompt_buf

    def stage_norm(src, dst, g_sb, tile):
        n = tile.n
        ps = PS()
        for kc in range(8):
            sq = tb.get()
            P.act(sq[:, 0:n], tile.sl(src, kc), AF.Square)
            P.mm(ps[:, 0:n], lhsT=ones_b[:], rhs=sq[:, 0:n], start=(kc == 0), stop=(kc == 7))
        t1 = tf.get()
        P.act(t1[:, 0:n], ps[:, 0:n], AF.Sqrt, scale=1.0 / D, bias=eps_t[:, 0:1])
        rstd = tf.get()
        P.recip(rstd[:, 0:n], t1[:, 0:n])
        for kc in range(8):
            P.stt(tile.sl(dst, kc), tile.sl(src, kc), g_sb[:, kc:kc + 1], rstd[:, 0:n], ALU.mult, ALU.mult)

    def rs_small(ss, m, w, inv_n):
        s2 = tsm.get()
        P.ts("vector", s2[0:m, 0:w], ss, inv_n, EPS, ALU.mult, ALU.add)
        rs = tsm.get()
        P.tt("gpsimd", rs[0:m, 0:w], s2[0:m, 0:w], mhalf[0:m, 0:1].to_broadcast([m, w]), ALU.pow)
        return rs

    def q_mm(slA, slB, xn_cols, m, pipelined=False):
        ps = psq.get() if pipelined else PS()
        for kc in range(8):
            P.mm(ps[0:m, 0:384], lhsT=xn_cols(kc), rhs=(slA[:, kc, :] if kc < 5 else slB[:, kc - 5, :]),
                 start=(kc == 0), stop=(kc == 7))
        return ps

    def q_chA(ps, m):
        sq = tfh.get()
        P.act(sq[0:m, :], ps[0:m, 0:256], AF.Square)
        ss = tsm.get()
        P.reduce(ss[0:m, 0:4], sq[0:m, :].rearrange("p (h d) -> p h d", h=4))
        rs = rs_small(ss[0:m, 0:4], m, 4, 1.0 / 64)
        return (ps, sq, rs)

    def q_chB(stA, m, cosv, sinv, vdst0, vdst1):
        ps, qn, rs = stA
        P.tt("vector", qn[0:m, :].rearrange("p (h d) -> p h d", h=4),
             ps[0:m, 0:256].rearrange("p (h d) -> p h d", h=4),
             rs[0:m, 0:4].unsqueeze(2).to_broadcast([m, 4, 64]), ALU.mult)
        P.copy("scalar", vdst0, ps[0:m, 256:320])
        P.copy("scalar", vdst1, ps[0:m, 320:384])
        stg = tfh.get()
        P.copy("scalar", stg[0:m, 128:256], ps[0:m, 256:384])
        P.tt("vector", qn[0:m, :], qn[0:m, :], gqk_sb[0:m, :], ALU.mult)
        qv = qn[0:m, :].rearrange("p (h two d) -> p h two d", h=4, two=2)
        ra = tfh.get()
        rav = ra[0:m, :].rearrange("p (h two d) -> p h two d", h=4, two=2)
        cosb = cosv.unsqueeze(1).unsqueeze(1).to_broadcast([m, 4, 2, 32])
        sinb = sinv.unsqueeze(1).to_broadcast([m, 4, 32])
        P.tt("vector", rav, qv, cosb, ALU.mult)
        rb = tfh.get()
        rbv = rb[0:m, :].rearrange("p (h two d) -> p h two d", h=4, two=2)
        P.tt("gpsimd", rbv[:, :, 1, :], qv[:, :, 0, :], sinb, ALU.mult)
        P.stt(rbv[:, :, 0, :], qv[:, :, 1, :], -1.0, sinb, ALU.mult, ALU.mult)
        return (stg, ra, rb)

    def q_chC(stB, m, kv_dst):
        stg, ra, rb = stB
        P.tt("vector", stg[0:m, 0:128], ra[0:m, 128:256], rb[0:m, 128:256], ALU.add)
        if kv_dst is not None:
            P.dma("sync", kv_dst[0], stg[0:m, 0:128])
            P.dma("sync", kv_dst[1], stg[0:m, 128:256])
        qk = tbh.get()
        P.tt("gpsimd", qk[0:m, 0:128], ra[0:m, 0:128], rb[0:m, 0:128], ALU.add)
        P.copy("gpsimd", qk[0:m, 128:256], stg[0:m, 0:128])
        return qk

    def q_chain(ps, m, cosv, sinv, kv_dst, vdst0, vdst1):
        stA = q_chA(ps, m)
        stB = q_chB(stA, m, cosv, sinv, vdst0, vdst1)
        return q_chC(stB, m, kv_dst)

    def q_tr(qk, m, qdst, kdst):
        pt = PSB()
        P.tr(pt[:, 0:m], qk[0:m, 0:128], ident_b[0:m, 0:m])
        P.tr(pt[:, 128:128 + m], qk[0:m, 128:256], ident_b[0:m, 0:m])
        P.copy("scalar", qdst, pt[:, 0:m])
        P.copy("scalar", kdst, pt[:, 128:128 + m])

    def qkv_block(slA, slB, xn_cols, m, cosv, sinv, kv_dst, vdst0, vdst1, qdst, kdst):
        ps = q_mm(slA, slB, xn_cols, m)
        qk = q_chain(ps, m, cosv, sinv, kv_dst, vdst0, vdst1)
        q_tr(qk, m, qdst, kdst)

    def finalize_heads(accv, dstv, hp, n0, n, small=False):
        if small:
            tmp, rd = fz_tmp, fz_rd
            P.copy("vector", tmp[0:64, 0:n], accv[64:128, 0, n0:n0 + n])
            P.copy("vector", tmp[64:128, 0:n], accv[0:64, 1, n0:n0 + n])
            P.recip(rd[:, 0:n], tmp[:, 0:n])
            P.tt("vector", dstv[0:64, hp, n0:n0 + n], accv[0:64, 0, n0:n0 + n], rd[0:64, 0:n], ALU.mult)
            P.tt("vector", dstv[64:128, hp, n0:n0 + n], accv[64:128, 1, n0:n0 + n], rd[64:128, 0:n], ALU.mult)
            return
        tmp = tf.get()
        P.copy("vector", tmp[0:64, 0:n], accv[64:128, 0, n0:n0 + n])
        P.copy("vector", tmp[64:128, 0:n], accv[0:64, 1, n0:n0 + n])
        rd = tf.get()
        P.recip(rd[:, 0:n], tmp[:, 0:n])
        P.tt("vector", dstv[0:64, hp, n0:n0 + n], accv[0:64, 0, n0:n0 + n], rd[0:64, 0:n], ALU.mult)
        P.tt("vector", dstv[64:128, hp, n0:n0 + n], accv[64:128, 1, n0:n0 + n], rd[64:128, 0:n], ALU.mult)

    n_layers = DEV.get("layers", L)
    n_seq = DEV.get("seqs", NSEQ)

    def record_all():
        P.dma("sync", ident_f[:], c_ident[:, :])
        P.dma("gpsimd", ident_b[:], c_ident[:, :])
        P.dma("gpsimd", tril_b[:], c_tril[:, :])
        P.dma("gpsimd", mask_b[:], c_mask[:, :])
        P.dma("gpsimd", masks_b[:], c_masks[:, :])
        P.dma("sync", cs_s[:], c_cs_s[:, :])
        P.memset("gpsimd", ones_b[:], 1.0)
        P.memset("gpsimd", mhalf[:], -0.5)
        P.memset("gpsimd", eps_t[:], EPS)
        P.memset("gpsimd", zeros2[:], 0.0)
        P.memset("gpsimd", vnew[:], 1.0)
        for b in range(2):
            P.memset("gpsimd", Vc[b][:], 1.0)
        for s in range(n_seq):
            with_sample = (s == 0) and DEV.get("sample", True)
            record_seq(s, with_sample)

    xpend = {}

    def issue_x_block(s, blk):
        xi = xin.get()
        P.dma("sync", xi, xp[s, blk * 128:(blk + 1) * 128, :])
        xpend[(s, blk)] = xi

    def load_x_block(s, blk):
        if (s, blk) not in xpend:
            issue_x_block(s, blk)
        xi = xpend.pop((s, blk))
        for half in range(2):
            ps = PS()
            for k4 in range(4):
                kc = half * 4 + k4
                P.tr(ps[:, k4 * 128:(k4 + 1) * 128], xi[:, kc * 128:(kc + 1) * 128], ident_f[:])
            P.copy("scalar" if half else "vector", xT[:, half * 4:(half + 1) * 4, blk * 128:(blk + 1) * 128],
                   ps[:].rearrange("p (k t) -> p k t", k=4))

    def store_y_block(s, blk):
        xo = xin.get()
        for half in range(2):
            ps = PS()
            for k4 in range(4):
                kc = half * 4 + k4
                P.tr(ps[:, k4 * 128:(k4 + 1) * 128], xT[:, kc, blk * 128:(blk + 1) * 128], ident_f[:])
            P.copy("scalar" if half else "vector", xo[:, half * 512:(half + 1) * 512], ps[:])
        P.dma("sync", y[s, blk * 128:(blk + 1) * 128, :], xo)

    def record_seq(s, with_sample):
        if s == 0:
            for blk in range(16):
                for ahead in range(blk, min(blk + 4, 16)):
                    if (s, ahead) not in xpend:
                        issue_x_block(s, ahead)
                load_x_block(s, blk)
        if with_sample:
            xi = xin.get()
            P.dma("sync", xi[0:NS, :], xs[:, :])
            for half in range(2):
                ps = PS()
                for k4 in range(4):
                    kc = half * 4 + k4
                    P.tr(ps[:, k4 * NS:(k4 + 1) * NS], xi[0:NS, kc * 128:(kc + 1) * 128], ident_f[0:NS, 0:NS])
                P.copy("vector", xTs[:, half * 4:(half + 1) * 4, :],
                       ps[:, 0:4 * NS].rearrange("p (k t) -> p k t", k=4))
        for l in range(n_layers):
            record_layer(s, l, with_sample)
        for blk in range(16):
            if s + 1 < n_seq:
                for ahead in range(blk, min(blk + 3, 16)):
                    if (s + 1, ahead) not in xpend:
                        issue_x_block(s + 1, ahead)
            store_y_block(s, blk)
            if s + 1 < n_seq:
                load_x_block(s + 1, blk)
        if with_sample:
            xo = xin.get()
            for half in range(2):
                ps = PS()
                for k4 in range(4):
                    kc = half * 4 + k4
                    P.tr(ps[0:NS, k4 * 128:(k4 + 1) * 128], xTs[:, kc, :], ident_f[:])
                P.copy("vector", xo[0:NS, half * 512:(half + 1) * 512], ps[0:NS, :])
            P.dma("sync", ys[:, :], xo[0:NS, :])

    def record_layer(s, l, with_sample):
        load_layer_params(l)
        tiles = [Tile(False, i) for i in range(4)] + ([Tile(True, 0)] if with_sample else [])

        for tile in tiles:
            stage_norm(buf(tile, xT, xTs), buf(tile, xnT, xnTs), gm_sb, tile)

        if DEV.get("gmlp", True):
            for uh in range(2):
                sl = load_slab(lambda t, uh=uh: [(t[:].rearrange("p (k c) -> p k c", k=8),
                                                  w_in[l, :, uh * 256:(uh + 1) * 256].rearrange("(k p) c -> p k c", p=128))])
                slv = sl[:].rearrange("p (k c) -> p k c", k=8)
                for o2 in range(2):
                    oc = uh * 2 + o2
                    for tile in tiles:
                        n = tile.n
                        ps = PS()
                        for kc in range(8):
                            P.mm(ps[:, 0:n], lhsT=slv[:, kc, o2 * 128:(o2 + 1) * 128],
                                 rhs=tile.sl(buf(tile, xnT, xnTs), kc), start=(kc == 0), stop=(kc == 7))
                        P.act(tile.sl(buf(tile, aT, aTs), oc), ps[:, 0:n], AF.Gelu_apprx_tanh)
            slv2 = []
            for vh in range(2):
                sl = load_slab(lambda t, vh=vh: [(t[:].rearrange("p (k c) -> p k c", k=4),
                                                  w_in[l, vh * 512:(vh + 1) * 512, 512:1024].rearrange("(k p) c -> p k c", p=128))])
                slv2.append(sl[:].rearrange("p (k c) -> p k c", k=4))

            def v_mm(xn_cols, m):
                ps = PS()
                for kc in range(8):
                    P.mm(ps[0:m, :], lhsT=xn_cols(kc), rhs=slv2[kc // 4][:, kc % 4, :],
                         start=(kc == 0), stop=(kc == 7))
                return ps

            def v_chain(ps, m):
                gvf = tf.get()
                P.act(gvf[0:m, :], ps[0:m, :], AF.Gelu_apprx_tanh)
                junk = tf.get()
                ss = tsm.get()
                P.act(junk[0:m, :], gvf[0:m, :], AF.Square, accum_out=ss[0:m, 0:1])
                rs = rs_small(ss[0:m, 0:1], m, 1, 1.0 / 512)
                return gvf, rs

            def v_branch(xn_cols, m):
                return v_chain(v_mm(xn_cols, m), m)

            def v_stage2(blk, ps):
                gvf, rs = v_chain(ps, 128)
                va = tb.get()
                P.stt(va[:], gvf[:], rs[:, 0:1], gv_sb[:], ALU.mult, ALU.mult)
                return va

            def v_stage3(blk, va):
                ps = PS()
                for g in range(8):
                    j, hf = g // 2, g % 2
                    P.mm(ps[hf * 64:(hf + 1) * 64, j * 128:(j + 1) * 128],
                         lhsT=va[:, g * 64:(g + 1) * 64], rhs=ws_b[:, g, :], start=True, stop=True)
                zb = tf.get()
                zv = zb[:].rearrange("p (j t) -> p j t", j=4)
                P.tt("vector", zv, ps[:].rearrange("p (j t) -> p j t", j=4), bT_sb, ALU.add)
                av = aT[:, :, blk * 128:(blk + 1) * 128]
                P.tt("gpsimd", av, zv, av, ALU.mult)

            vst = {}
            for step in range(16 + 2):
                if step < 16:
                    vst[step] = v_mm(lambda kc, step=step: xnT[:, kc, step * 128:(step + 1) * 128], 128)
                if 0 <= step - 1 < 16:
                    vst[step - 1] = v_stage2(step - 1, vst[step - 1])
                if 0 <= step - 2 < 16:
                    v_stage3(step - 2, vst[step - 2])
            if with_sample:
                gvf, rs = v_branch(lambda kc: xnTs[:, kc, :], NS)
                vaf = tf.get()
                P.stt(vaf[0:NS, :], gvf[0:NS, :], rs[0:NS, 0:1], gv_sb[0:NS, :], ALU.mult, ALU.mult)
                P.dma("sync", vch[l], vaf[0:NS, :])
                ps = PS()
                for j in range(4):
                    P.tr(ps[:, j * NS:(j + 1) * NS], vaf[0:NS, j * 128:(j + 1) * 128], ident_f[0:NS, 0:NS])
                zb = tf.get()
                zv = zb[:, 0:4 * NS].rearrange("p (j t) -> p j t", j=4)
                P.tt("vector", zv, ps[:, 0:4 * NS].rearrange("p (j t) -> p j t", j=4),
                     w00_sb[:].unsqueeze(2).to_broadcast([128, 4, NS]), ALU.mult)
                zb2 = tf.get()
                zv2 = zb2[:, 0:4 * NS].rearrange("p (j t) -> p j t", j=4)
                P.tt("vector", zv2, zv, b00_sb[:].unsqueeze(2).to_broadcast([128, 4, NS]), ALU.add)
                P.tt("vector", aTs[:], zv2, aTs[:], ALU.mult)

        if DEV.get("attn", True):
            glist = [(hp, i) for hp in range(2) for i in range(3)][:DEV.get("ngroups", 6)]
            tabs = [(cos_sb[:].rearrange("p a b -> p (a b)"), sin_sb[:].rearrange("p a b -> p (a b)")),
                    (gpar[:, 0:512], gpar[:, 512:1024])]
            GS = {}

            def qkv_slabs(g):
                slA = load_slab(lambda t, g=g: [(t[:, 0:1920].rearrange("p (k c) -> p k c", k=5),
                                                 wqkv[l, 0:640, g * 384:(g + 1) * 384].rearrange("(k p) c -> p k c", p=128))])
                slB = load_slab(lambda t, g=g: [(t[:, 0:1152].rearrange("p (k c) -> p k c", k=3),
                                                 wqkv[l, 640:1024, g * 384:(g + 1) * 384].rearrange("(k p) c -> p k c", p=128))])
                return (slA[:, 0:1920].rearrange("p (k c) -> p k c", k=5),
                        slB[:, 0:1152].rearrange("p (k c) -> p k c", k=3))

            def open_group(gi_):
                hp, i = glist[gi_]
                win, dil = PAIRS[i]
                slA, slB = qkv_slabs(i * 2 + hp)
                cf, sf = tabs[gi_ % 2]
                P.dma("sync", cf, c_cos[i])
                P.dma("sync", sf, c_sin[i])
                if gi_ == 0:
                    P.memset("gpsimd", Vb[:, :, 0, 64:128], 1.0)
                    P.memset("gpsimd", Vb[:, :, 1, 0:64], 1.0)
                GS[gi_] = dict(hp=hp, i=i, win=win, dil=dil, slA=slA, slB=slB,
                               cos=cf.rearrange("p (a b) -> p a b", a=16), sin=sf.rearrange("p (a b) -> p a b", a=16))

            def blk_info(G, blk):
                dil, win, hp, i = G["dil"], G["win"], G["hp"], G["i"]
                n_, r_ = blk // dil, blk % dil
                base = dil * 128 * n_ + r_
                kv_dst = None
                t0 = T - win
                if base >= t0:
                    row0 = base - t0
                    dd = kvp[i][l, s]
                    kd = bass.AP(dd.tensor, dd.offset + row0 * 512 + hp * 128, [[dil * 512, 128], [1, 128]])
                    vd = bass.AP(dd.tensor, dd.offset + row0 * 512 + 256 + hp * 128, [[dil * 512, 128], [1, 128]])
                    kv_dst = (kd, vd)
                tok = (lambda kc, base=base, dil=dil: xnT[:, kc, base:base + 127 * dil + 1:dil])
                return tok, kv_dst, base

            def st_mm(G, blk):
                tok, _, _ = blk_info(G, blk)
                return q_mm(G["slA"], G["slB"], tok, 128, pipelined=True)

            def st_A(G, blk, ps):
                return q_chA(ps, 128)

            def st_B(G, blk, stA):
                return q_chB(stA, 128, G["cos"][:, blk, :], G["sin"][:, blk, :],
                             Vb[:, blk, 0, 0:64], Vb[:, blk, 1, 64:128])

            def st_C(G, blk, stB):
                _, kv_dst, _ = blk_info(G, blk)
                return q_chC(stB, 128, kv_dst)

            def st_tr(G, blk, qk):
                q_tr(qk, 128, qTb[:, blk * 128:(blk + 1) * 128], kTb[:, blk * 128:(blk + 1) * 128])

            def key_blocks(G, blk):
                dil = G["dil"]
                return ([blk - dil] if blk // dil >= 1 else []) + [blk]

            def st_qk(G, blk):
                kbs = key_blocks(G, blk)
                nk = len(kbs)
                pT = tbp.get()
                for hh in range(2):
                    pss = psa.get()
                    if nk == 2:
                        P.mm(pss[:, 0:256], lhsT=ident_b[:], rhs=mask_b[:, 0:256], start=True, stop=False)
                    else:
                        P.mm(pss[:, 0:128], lhsT=ident_b[:], rhs=mask_b[:, 128:256], start=True, stop=False)
                    for ki, kb in enumerate(kbs):
                        P.mm(pss[:, ki * 128:(ki + 1) * 128],
                             lhsT=kTb[hh * 64:(hh + 1) * 64, kb * 128:(kb + 1) * 128],
                             rhs=qTb[hh * 64:(hh + 1) * 64, blk * 128:(blk + 1) * 128],
                             start=False, stop=(ki == nk - 1))
                    P.act(pT[:, hh * nk * 128:(hh + 1) * nk * 128], pss[:, 0:nk * 128], AF.Exp, scale=0.125)
                return pT

            def st_pv(G, blk, pT):
                kbs = key_blocks(G, blk)
                nk = len(kbs)
                dil = G["dil"]
                _, _, base = blk_info(G, blk)
                pso = psa.get()
                for hh in range(2):
                    for ki, kb in enumerate(kbs):
                        c0 = (hh * nk + ki) * 128
                        P.mm(pso[:, hh * 128:(hh + 1) * 128], lhsT=Vb[:, kb, hh, :],
                             rhs=pT[:, c0:c0 + 128], start=(ki == 0), stop=(ki == nk - 1))
                dst = acc[:, :, base:base + 127 * dil + 1:dil]
                src = pso[:, 0:256].rearrange("p (h q) -> p h q", h=2)
                if G["i"] == 0:
                    P.copy("scalar", dst, src)
                else:
                    P.tt("vector", dst, src, dst, ALU.add)

            NV = len(glist) * 16
            stv = {}
            stages = [st_mm, st_A, st_B, st_C, st_tr, st_qk, st_pv]
            for step in range(NV + 6):
                for si in (0, 3, 1, 2, 4, 5, 6):
                    fn = stages[si]
                    v = step - si
                    if not (0 <= v < NV):
                        continue
                    gi_, blk = divmod(v, 16)
                    if si == 0 and blk == 0:
                        open_group(gi_)
                    G = GS[gi_]
                    if si == 0:
                        stv[v] = fn(G, blk)
                    elif si in (4, 6):
                        fn(G, blk, stv[v])
                    elif si == 5:
                        stv[v] = fn(G, blk)
                    else:
                        stv[v] = fn(G, blk, stv[v])
                c = step - (16 * 3 + 5)
                if len(glist) == 6 and 0 <= c < 16:
                    finalize_heads(acc, bTo, 0, c * 128, 128, small=True)
            if len(glist) == 6:
                for tl in range(4):
                    finalize_heads(acc, bTo, 1, tl * 512, 512)

            if with_sample:
                for hp in range(2):
                    for i, (win, dil) in enumerate(PAIRS):
                        slA, slB = qkv_slabs(i * 2 + hp)

                        def load_cache(b, i=i, dil=dil, hp=hp):
                            kc_ = kcb[b % 2]
                            vc_ = Vc[b % 2]
                            src = ck[i][l, b]
                            P.dma("gpsimd", kc_[:], bass.AP(src.tensor, src.offset + hp * 128, [[dil * 512, 128], [1, 128]]))
                            P.dma("gpsimd", vc_[:, 0, 0:64],
                                  bass.AP(src.tensor, src.offset + 256 + hp * 128, [[dil * 512, 128], [1, 64]]))
                            P.dma("gpsimd", vc_[:, 1, 64:128],
                                  bass.AP(src.tensor, src.offset + 256 + hp * 128 + 64, [[dil * 512, 128], [1, 64]]))

                        load_cache(0)
                        load_cache(1)
                        qkv_block(slA, slB, lambda kc: xnTs[:, kc, :], NS, cs_s[0:NS, 0:32], cs_s[0:NS, 32:64],
                                  (kvs[i][l, :, hp * 128:(hp + 1) * 128], kvs[i][l, :, 256 + hp * 128:256 + (hp + 1) * 128]),
                                  vnew[0:NS, 0, 0:64], vnew[0:NS, 1, 64:128], qTs[:, :], kTs[:, :])
                        for hh in range(2):
                            pss = PS()
                            P.mm(pss[0:NS, 0:4], lhsT=ident_b[0:NS, 0:NS], rhs=masks_b[0:NS, 32 + hh * 4:36 + hh * 4],
                                 start=True, stop=False)
                            P.mm(pss[0:NS, 0:4], lhsT=kTs[hh * 64:(hh + 1) * 64, :],
                                 rhs=qTs[hh * 64:(hh + 1) * 64, :], start=False, stop=True)
                            P.act(pTs[0:NS, NS, hh * 4:(hh + 1) * 4], pss[0:NS, 0:4], AF.Exp, scale=0.125)
                        first = [i == 0]

                        def acc_add(pso):
                            srcs = pso[:, 0:8].rearrange("p (h q) -> p h q", h=2)
                            if first[0]:
                                P.copy("vector", accs[:], srcs)
                                first[0] = False
                            else:
                                P.tt("vector", accs[:], srcs, accs[:], ALU.add)

                        pso = PS()
                        for hh in range(2):
                            P.mm(pso[:, hh * 4:(hh + 1) * 4], lhsT=vnew[0:NS, hh, :], rhs=pTs[0:NS, NS, hh * 4:(hh + 1) * 4],
                                 start=True, stop=True)
                        acc_add(pso)
                        for b in range(NS):
                            kc_ = kcb[b % 2]
                            vc_ = Vc[b % 2]
                            pt = PSB()
                            P.tr(pt[:, 0:128], kc_[:], ident_b[:])
                            kT_ = kcT[b % 2]
                            P.copy("vector", kT_[:], pt[:, 0:128])
                            for hh in range(2):
                                pss = PS()
                                P.mm(pss[:, 0:4], lhsT=ident_b[:], rhs=masks_b[:, b * 8 + hh * 4:b * 8 + hh * 4 + 4],
                                     start=True, stop=False)
                                P.mm(pss[:, 0:4], lhsT=kT_[hh * 64:(hh + 1) * 64, :],
                                     rhs=qTs[hh * 64:(hh + 1) * 64, :], start=False, stop=True)
                                P.act(pTs[:, b, hh * 4:(hh + 1) * 4], pss[:, 0:4], AF.Exp, scale=0.125)
                            pso = PS()
                            for hh in range(2):
                                P.mm(pso[:, hh * 4:(hh + 1) * 4], lhsT=vc_[:, hh, :], rhs=pTs[:, b, hh * 4:(hh + 1) * 4],
                                     start=True, stop=True)
                            acc_add(pso)
                            if b + 2 < NS:
                                load_cache(b + 2)
                    finalize_heads(accs, bTos, hp, 0, NS)

        if DEV.get("merge", True):
            for jc in range(8):
                sga_ = load_slab(lambda t, jc=jc: [(t[:, 0:1024], wmrg[l, jc][:, 0:1024])])
                sgb_ = load_slab(lambda t, jc=jc: [(t[:, 0:1024], wmrg[l, jc][:, 1024:2048])])
                sab_ = load_slab(lambda t, jc=jc: [(t[:, 0:768], wmrg[l, jc][:, 2048:2816])])
                wga = sga_[:, 0:1024].rearrange("p (k c) -> p k c", k=8)
                wgb = sgb_[:, 0:1024].rearrange("p (k c) -> p k c", k=8)
                wab = sab_[:, 0:768].rearrange("p (k c) -> p k c", k=6)
                for tile in tiles:
                    n = tile.n
                    pga, pgb, pa, pb = PS(), PS(), PS(), PS()
                    xn_ = buf(tile, xnT, xnTs)
                    for kc in range(8):
                        P.mm(pga[:, 0:n], lhsT=wga[:, kc, :], rhs=tile.sl(xn_, kc), start=(kc == 0), stop=(kc == 7))
                    for kc in range(8):
                        P.mm(pgb[:, 0:n], lhsT=wgb[:, kc, :], rhs=tile.sl(xn_, kc), start=(kc == 0), stop=(kc == 7))
                    for kc in range(4):
                        P.mm(pa[:, 0:n], lhsT=wab[:, kc, :], rhs=tile.sl(buf(tile, aT, aTs), kc),
                             start=(kc == 0), stop=(kc == 3))
                    for kc in range(2):
                        P.mm(pb[:, 0:n], lhsT=wab[:, 4 + kc, :], rhs=tile.sl(buf(tile, bTo, bTos), kc),
                             start=(kc == 0), stop=(kc == 1))
                    sga, sgb = tf.get(), tf.get()
                    P.act(sga[:, 0:n], pga[:, 0:n], AF.Sigmoid)
                    P.act(sgb[:, 0:n], pgb[:, 0:n], AF.Sigmoid)
                    P.tt("vector", sga[:, 0:n], sga[:, 0:n], pa[:, 0:n], ALU.mult)
                    P.tt("vector", sgb[:, 0:n], sgb[:, 0:n], pb[:, 0:n], ALU.mult)
                    P.tt("gpsimd", tile.sl(buf(tile, mT, mTs), jc), sga[:, 0:n], sgb[:, 0:n], ALU.add)
            for oc in range(8):
                sl = load_slab(lambda t, oc=oc: [(t[:, 0:1024], wo_r[l, oc])])
                slv = sl[:, 0:1024].rearrange("p (k c) -> p k c", k=8)
                for tile in tiles:
                    n = tile.n
                    ps = PS()
                    for kc in range(8):
                        P.mm(ps[:, 0:n], lhsT=slv[:, kc, :], rhs=tile.sl(buf(tile, mT, mTs), kc),
                             start=(kc == 0), stop=(kc == 7))
                    xd = tile.sl(buf(tile, xT, xTs), oc)
                    P.tt("vector", xd, ps[:, 0:n], xd, ALU.add)

        if DEV.get("ffn", True):
            for tile in tiles:
                stage_norm(buf(tile, xT, xTs), buf(tile, xnT, xnTs), gf_sb, tile)
            if with_sample:
                P.dma("sync", cvs[l, :, 0, :], stc[l, :, 1, :])
                for q4 in range(11):
                    stg = tf.get()
                    P.dma("sync", stg[0:2 * NS, :], stc[l, :, :, q4 * 512:(q4 + 1) * 512].rearrange("b j c -> (b j) c"))
                    ps = PS()
                    for k4 in range(4):
                        P.tr(ps[:, k4 * 8:(k4 + 1) * 8], stg[0:2 * NS, k4 * 128:(k4 + 1) * 128], ident_f[0:2 * NS, 0:2 * NS])
                    for k4 in range(4):
                        P.copy("vector", stTs[:, q4 * 4 + k4, :, :],
                               ps[:, k4 * 8:(k4 + 1) * 8].rearrange("p (b j) -> p j b", j=2))
            for gi in range(2):
                P.copy("scalar", cry[gi][0][:, 0:2], zeros2[:])
            groups = [(0, 8), (8, 8), (16, 6)]
            for (c_lo, gn) in groups:
                def hsl(tile, c):
                    if tile.sample:
                        return hTs[:, c, :]
                    return mT[:, c, tile.i * 512:(tile.i + 1) * 512]

                for c in range(gn):
                    cc = c_lo + c
                    sl = load_slab(lambda t, cc=cc: [(t[:, 0:1024], wup_r[l, cc][:, 0:1024]), (t[:, 1024:2048], wup_r[l, cc][:, 1024:2048])])
                    slv = sl[:].rearrange("p (g k c) -> p g k c", g=2, k=8)
                    for tile in tiles:
                        n = tile.n
                        cres = []
                        for gi in range(2):
                            ch = gi * NCH + cc
                            ps = PS()
                            for kc in range(8):
                                P.mm(ps[:, 0:n], lhsT=slv[:, gi, kc, :], rhs=tile.sl(buf(tile, xnT, xnTs), kc),
                                     start=(kc == 0), stop=(kc == 7))
                            c0 = tf.get()
                            P.act(c0[:, 0:n], ps[:, 0:n], AF.Identity, scale=cw_sb[:, 2, ch:ch + 1],
                                  bias=cb_sb[:, ch:ch + 1])
                            if tile.sample:
                                P.copy("scalar", upls[:, ch, :], ps[:, 0:n])
                                P.stt(c0[:, 0:n], stTs[:, ch, 1, :], cw_sb[:, 1, ch:ch + 1], c0[:, 0:n], ALU.mult, ALU.add)
                                P.stt(c0[:, 0:n], stTs[:, ch, 0, :], cw_sb[:, 0, ch:ch + 1], c0[:, 0:n], ALU.mult, ALU.add)
                            else:
                                cr = cry[gi][tile.i % 2]
                                nxt = cry[gi][(tile.i + 1) % 2]
                                w1c, w0c = cw_sb[:, 1, ch:ch + 1], cw_sb[:, 0, ch:ch + 1]
                                if tile.i == 3:
                                    P.copy("scalar", upl[:, ch, :], ps[:, 510:512])
                                    P.copy("scalar", nxt[:, 0:2], zeros2[:])
                                else:
                                    P.act(nxt[:, 0:2], ps[:, 510:512], AF.Copy, scale=w0c)
                                    P.act(nxt[:, 0:1], ps[:, 511:512], AF.Identity, scale=w1c, bias=nxt[:, 0:1])
                                P.stt(c0[:, 1:512], ps[:, 0:511], w1c, c0[:, 1:512], ALU.mult, ALU.add)
                                P.stt(c0[:, 2:512], ps[:, 0:510], w0c, c0[:, 2:512], ALU.mult, ALU.add)
                                P.tt("vector", c0[:, 0:2], c0[:, 0:2], cr[:, 0:2], ALU.add)
                            cres.append(c0)
                        sg = tb.get()
                        P.act(sg[:, 0:n], cres[0][:, 0:n], AF.Silu)
                        P.tt("gpsimd", hsl(tile, c), sg[:, 0:n], cres[1][:, 0:n], ALU.mult)
                for oc in range(8):
                    def mk(t, oc=oc, c_lo=c_lo, gn=gn):
                        a = w_dn[l]
                        src = bass.AP(a.tensor, a.offset + c_lo * 128 * D + oc * 128, [[D, 128], [128 * D, gn], [1, 128]])
                        return [(t[:, 0:gn * 128].rearrange("p (c k) -> p c k", c=gn), src)]
                    sl = load_slab(mk)
                    slv = sl[:, 0:gn * 128].rearrange("p (c k) -> p c k", c=gn)
                    for tile in tiles:
                        n = tile.n
                        ps = PS()
                        for c in range(gn):
                            P.mm(ps[:, 0:n], lhsT=slv[:, c, :], rhs=hsl(tile, c), start=(c == 0), stop=(c == gn - 1))
                        xd = tile.sl(buf(tile, xT, xTs), oc)
                        P.tt("vector", xd, ps[:, 0:n], xd, ALU.add)
            for q4 in range(11):
                ps = PS()
                for k4 in range(4):
                    P.tr(ps[0:2, k4 * 128:(k4 + 1) * 128], upl[:, q4 * 4 + k4, :], ident_f[:])
                stg = tf.get()
                P.copy("vector", stg[0:2, :], ps[0:2, :])
                P.dma("sync", cvp[l, s, :, q4 * 512:(q4 + 1) * 512], stg[0:2, :])
                if with_sample:
                    ps = PS()
                    for k4 in range(4):
                        P.tr(ps[0:NS, k4 * 128:(k4 + 1) * 128], upls[:, q4 * 4 + k4, :], ident_f[:])
                    stg = tf.get()
                    P.copy("vector", stg[0:NS, :], ps[0:NS, :])
                    P.dma("sync", cvs[l, :, 1, q4 * 512:(q4 + 1) * 512], stg[0:NS, :])

    P.dry = True
    st["dry"] = True
    record_all()
    P.dry = False
    st["dry"] = False
    st["seq"] = 0
    record_all()

    P.emit_all(es)
    es.close()
    return nc


def _host_consts():
    ident = np.eye(128, dtype=np.float32)
    tril = (np.arange(128)[:, None] <= np.arange(128)[None, :]).astype(np.float32)
    k = np.arange(128)[:, None]
    q = np.arange(128)[None, :]
    prev = np.where(k >= q, 0.0, NEG).astype(np.float32)
    cur = np.where(k <= q, 0.0, NEG).astype(np.float32)
    one = np.concatenate([prev, cur], axis=1)
    mask = np.concatenate([one, one], axis=1)
    masks = np.full((128, 40), NEG, np.float32)
    for b in range(NS):
        for hh in range(2):
            masks[:, b * 8 + hh * 4 + b] = 0.0
    for hh in range(2):
        for b in range(NS):
            masks[b, 32 + hh * 4 + b] = 0.0
    half = 32
    inv = (10000.0 ** (-np.arange(half, dtype=np.float32) / half)).astype(np.float32)
    cos = np.zeros((3, 128, 16, 32), np.float32)
    sin = np.zeros((3, 128, 16, 32), np.float32)
    for i, (win, dil) in enumerate(PAIRS):
        for blk in range(16):
            n_, r_ = blk // dil, blk % dil
            pos = (dil * 128 * n_ + r_ + dil * np.arange(128)).astype(np.float32)
            ang = pos[:, None] * inv[None, :]
            c, s_ = np.cos(ang), np.sin(ang)
            cos[i, :, blk, :] = c
            sin[i, :, blk, :] = s_
    ang = np.float32(PAST) * inv
    cs = np.concatenate([np.cos(ang), np.sin(ang)]).astype(np.float32)
    cs_s = np.tile(cs[None, :], (NS, 1))
    return dict(c_ident=ident, c_tril=tril, c_mask=mask, c_masks=masks,
                c_cos=cos.reshape(3, 128, 512), c_sin=sin.reshape(3, 128, 512), c_cs_s=cs_s)


def _host_weights(inp):
    f = lambda a: np.ascontiguousarray(np.asarray(a, dtype=np.float32))
    w_in = f(inp["w_in"])
    o_q, o_k, o_v, o_ga, o_gb = 1024, 1792, 2560, 3328, 4352
    cols = []
    for g in range(6):
        i, hp = g // 2, g % 2
        h0 = (i * 4 + hp * 2) * 64
        for o in (o_q, o_k, o_v):
            cols.append(np.arange(o + h0, o + h0 + 128))
    cols = np.concatenate(cols)
    wqkv = f(w_in[:, :, cols])
    wa, wb = f(inp["w_a_proj"]), f(inp["w_b_proj"])
    wmrg = np.zeros((L, 8, 128, 22, 128), np.float32)
    for jc in range(8):
        cs_ = slice(jc * 128, (jc + 1) * 128)
        ga = w_in[:, :, o_ga:o_gb][:, :, cs_].reshape(L, 8, 128, 128)
        gb = w_in[:, :, o_gb:][:, :, cs_].reshape(L, 8, 128, 128)
        wmrg[:, jc, :, 0:8] = ga.transpose(0, 2, 1, 3)
        wmrg[:, jc, :, 8:16] = gb.transpose(0, 2, 1, 3)
        wmrg[:, jc, :, 16:20] = wa[:, :, cs_].reshape(L, 4, 128, 128).transpose(0, 2, 1, 3)
        wmrg[:, jc, :, 20:22] = wb[:, :, cs_].reshape(L, 2, 128, 128).transpose(0, 2, 1, 3)
    wmrg = wmrg.reshape(L, 8, 128, 2816)
    wo = f(inp["w_o"]).reshape(L, 8, 128, 8, 128)
    wo_r = f(wo.transpose(0, 3, 2, 1, 4)).reshape(L, 8, 128, 1024)
    wup = f(inp["w_up"]).reshape(L, 8, 128, 2, NCH, 128)
    wup_r = f(wup.transpose(0, 4, 2, 3, 1, 5)).reshape(L, NCH, 128, 2048)
    gm = f(f(inp["g_mix"]).reshape(L, 8, 128).transpose(0, 2, 1))
    gf = f(f(inp["g_ffn"]).reshape(L, 8, 128).transpose(0, 2, 1))
    gqk = np.concatenate([f(inp["g_q"])] * 2 + [f(inp["g_k"])] * 2, axis=1)
    ws_ = f(inp["w_s"])
    wsT = f(ws_.transpose(0, 3, 1, 2)).reshape(L, 128, 1024)
    bs = f(inp["b_s"])
    bT = f(np.repeat(bs.reshape(L, 4, 2, 1, 128), 64, axis=3).reshape(L, 4, 128, 128).transpose(0, 2, 1, 3))
    bT = bT.reshape(L, 128, 512)
    cwv = f(inp["conv_w"]).reshape(L, 3, 44, 128)
    cw = f(cwv.transpose(0, 3, 1, 2)).reshape(L, 128, 132)
    cb = f(f(inp["conv_b"]).reshape(L, 44, 128).transpose(0, 2, 1))
    w00 = f(np.repeat(ws_[:, :, 0, 0].reshape(L, 4, 2, 1), 64, axis=3).reshape(L, 4, 128).transpose(0, 2, 1))
    b00 = f(np.repeat(bs[:, :, 0].reshape(L, 4, 2, 1), 64, axis=3).reshape(L, 4, 128).transpose(0, 2, 1))
    return dict(w_in=w_in, wqkv=wqkv, wmrg=wmrg, wo_r=wo_r, wup_r=wup_r, w_dn=f(inp["w_down"]),
                gm=gm, gf=gf, gv=f(inp["g_v"]), gqk=f(gqk), wsT=wsT, bT=bT, cw=cw, cb=cb, w00=w00, b00=b00)


_NC_CACHE = {}


def kernel(**inp):
    shared = _host_weights(inp)
    shared.update(_host_consts())
    xp = np.asarray(inp["x_prompt"], np.float32)
    xs = np.asarray(inp["x_sample"], np.float32)
    cks = [np.asarray(inp[k], np.float32) for k in ("cache_kv_w128", "cache_kv_w512", "cache_kv_w2048")]
    stc = np.asarray(inp["state_conv"], np.float32)
    in_maps = []
    for c in range(NCORES):
        m = dict(shared)
        m["xp"] = np.ascontiguousarray(xp[2 * c:2 * c + 2])
        m["xs"] = np.ascontiguousarray(xs[4 * c:4 * c + 4, 0])
        for i in range(3):
            a = cks[i][:, 4 * c:4 * c + 4]
            m["ck%d" % i] = np.ascontiguousarray(a.reshape(L, NS, a.shape[2], 512))
        m["stc"] = np.ascontiguousarray(stc[:, 4 * c:4 * c + 4])
        in_maps.append(m)
    key = tuple(sorted(DEV.items()))
    if key not in _NC_CACHE:
        _NC_CACHE[key] = build_program()
    nc = _NC_CACHE[key]
    res = run_bass_kernel_spmd(nc, in_maps, core_ids=list(range(NCORES)))
    R = res.results
    cat = lambda name, ax: np.concatenate([np.asarray(r[name]) for r in R], axis=ax)
    y = cat("y", 0)
    ys = cat("ys", 0).reshape(32, 1, D)
    outs = [y, ys]
    for i in range(3):
        a = cat("kvp%d" % i, 1)
        outs.append(a.reshape(L, 16, PAIRS[i][0], 2, 4, 64))
    outs.append(cat("cvp", 1))
    for i in range(3):
        outs.append(cat("kvs%d" % i, 1).reshape(L, 32, 1, 2, 4, 64))
    outs.append(cat("cvs", 1))
    outs.append(cat("vch", 1).reshape(L, 32, 1, 512))
    return tuple(np.ascontiguousarray(o.astype(np.float32)) for o in outs)
```
